# Optimizing a Trainium2 kernel written in Bass

```python
import jax, jax.numpy as jnp
from jax import lax
import numpy as np

D_MODEL = 2048
BATCH = 2
SEQ = 16384
DEPTH = 2

N_MIXERS = 2
N_CONF = (DEPTH + 1) // 2
N_MAMBA = DEPTH // 2
CONF_KERNEL = 31
D_FF = 4 * D_MODEL
MB_EXPAND = 2
D_INNER = MB_EXPAND * D_MODEL
MB_HEADDIM = 64
MB_HEADS = D_INNER // MB_HEADDIM
MB_GROUPS = 8
MB_HPG = MB_HEADS // MB_GROUPS
MB_STATE = 128
MB_CONV = 4
MB_CHUNK = 128
MB_CONV_CH = D_INNER + 2 * MB_GROUPS * MB_STATE
MB_IN_COLS = 2 * D_INNER + 2 * MB_GROUPS * MB_STATE + MB_HEADS
EPS = 1e-6

kernel_name = "hybrid_conformer_conv_mamba2_ssd_adaln"


def rmsnorm(x, g):
    xf = x.astype(jnp.float32)
    y = xf * lax.rsqrt(jnp.mean(xf * xf, axis=-1, keepdims=True) + EPS)
    return (y * g.astype(jnp.float32)).astype(x.dtype)


def layernorm(x, g, b):
    xf = x.astype(jnp.float32)
    mu = jnp.mean(xf, axis=-1, keepdims=True)
    xc = xf - mu
    y = xc * lax.rsqrt(jnp.mean(xc * xc, axis=-1, keepdims=True) + EPS)
    return (y * g.astype(jnp.float32) + b.astype(jnp.float32)).astype(x.dtype)


def causal_dwconv(u, w, b):
    k, ch = w.shape
    out = lax.conv_general_dilated(
        u, w[:, None, :].astype(u.dtype), window_strides=(1,), padding=[(k - 1, 0)],
        dimension_numbers=("NWC", "WIO", "NWC"), feature_group_count=ch)
    return out + b


def conformer_conv(u, w_pw1, b_pw1, w_dw, b_dw, ln_g, ln_b, w_pw2, b_pw2):
    v = u @ w_pw1 + b_pw1
    a, gt = jnp.split(v, 2, axis=-1)
    v = a * jax.nn.sigmoid(gt)
    v = causal_dwconv(v, w_dw, b_dw)
    v = jax.nn.silu(layernorm(v, ln_g, ln_b))
    return v @ w_pw2 + b_pw2


def ssd_chunked(xs, dt, A, Bm, Cm):
    bsz, s = xs.shape[:2]
    nc = s // MB_CHUNK
    out_dtype = xs.dtype

    def to_chunks(t):
        t = t.astype(jnp.float32).reshape((bsz, nc, MB_CHUNK) + t.shape[2:])
        return jnp.moveaxis(t, 1, 0)

    tril = jnp.tril(jnp.ones((MB_CHUNK, MB_CHUNK), dtype=bool))[None, :, :, None, None]
    a_f = A.astype(jnp.float32)

    def step(state, inp):
        xc, dtc, bc, cc = inp
        acum = jnp.cumsum(dtc * a_f, axis=1)
        seg = acum[:, :, None] - acum[:, None, :]
        decay = jnp.exp(jnp.where(tril, seg, -jnp.inf))
        cb = jnp.einsum("blgn,bsgn->blsg", cc, bc)
        scores = cb[..., None] * decay * dtc[:, None]
        y_diag = jnp.einsum("blsgr,bsgrp->blgrp", scores, xc)
        y_off = jnp.einsum("blgn,bgrpn->blgrp", cc, state) * jnp.exp(acum)[..., None]
        w_end = jnp.exp(acum[:, -1:] - acum) * dtc
        new_state = state * jnp.exp(acum[:, -1])[..., None, None] + \
            jnp.einsum("bsgn,bsgr,bsgrp->bgrpn", bc, w_end, xc)
        return new_state, y_diag + y_off

    state0 = jnp.zeros((bsz, MB_GROUPS, MB_HPG, MB_HEADDIM, MB_STATE), jnp.float32)
    _, ys = lax.scan(step, state0, (to_chunks(xs), to_chunks(dt), to_chunks(Bm), to_chunks(Cm)))
    y = jnp.moveaxis(ys, 0, 1).reshape(xs.shape)
    return y.astype(out_dtype)


def mamba2(u, w_in, conv_w, conv_b, dt_bias, a_log, d_skip, norm_g, w_out):
    bsz, s, _ = u.shape
    zxbcdt = u @ w_in
    z = zxbcdt[..., :D_INNER]
    xbc = zxbcdt[..., D_INNER:D_INNER + MB_CONV_CH]
    dt_raw = zxbcdt[..., D_INNER + MB_CONV_CH:]
    xbc = jax.nn.silu(causal_dwconv(xbc, conv_w, conv_b))
    xs = xbc[..., :D_INNER].reshape(bsz, s, MB_GROUPS, MB_HPG, MB_HEADDIM)
    Bm = xbc[..., D_INNER:D_INNER + MB_GROUPS * MB_STATE].reshape(bsz, s, MB_GROUPS, MB_STATE)
    Cm = xbc[..., D_INNER + MB_GROUPS * MB_STATE:].reshape(bsz, s, MB_GROUPS, MB_STATE)
    dt = jax.nn.softplus(dt_raw.astype(jnp.float32) + dt_bias.astype(jnp.float32))
    dt = dt.reshape(bsz, s, MB_GROUPS, MB_HPG)
    A = -jnp.exp(a_log.astype(jnp.float32)).reshape(MB_GROUPS, MB_HPG)
    y = ssd_chunked(xs, dt, A, Bm, Cm) + d_skip.reshape(MB_GROUPS, MB_HPG)[..., None] * xs
    y = y.reshape(bsz, s, D_INNER) * jax.nn.silu(z)
    yg = y.reshape(bsz, s, MB_GROUPS, D_INNER // MB_GROUPS).astype(jnp.float32)
    yg = yg * lax.rsqrt(jnp.mean(yg * yg, axis=-1, keepdims=True) + EPS)
    y = (yg.reshape(bsz, s, D_INNER) * norm_g.astype(jnp.float32)).astype(u.dtype)
    return y @ w_out


def sq_relu_mlp(u, w1, w2):
    h = jax.nn.relu(u @ w1)
    return (h * h) @ w2


def setup_inputs(seed: int = 0) -> dict:
    key = jax.random.key(seed)
    ks = jax.random.split(key, 32)
    f32 = jnp.float32

    def nrm(k, shape, scale):
        return jax.random.normal(k, shape, f32) * scale

    dt0 = jnp.exp(jax.random.uniform(ks[20], (N_MAMBA, MB_HEADS), f32, np.log(1e-3), np.log(1e-1)))
    dt_bias = dt0 + jnp.log(-jnp.expm1(-dt0))
    a_log = jnp.log(jax.random.uniform(ks[21], (N_MAMBA, MB_HEADS), f32, 1.0, 16.0))
    return {
        "x": nrm(ks[0], (BATCH, SEQ, D_MODEL), 1.0),
        "c": nrm(ks[1], (BATCH, D_MODEL), 1.0),
        "w_mod": nrm(ks[2], (DEPTH, D_MODEL, 6 * D_MODEL), 0.5 * D_MODEL ** -0.5),
        "b_mod": nrm(ks[3], (DEPTH, 6 * D_MODEL), 0.02),
        "norm_mix_g": 1.0 + nrm(ks[4], (DEPTH, D_MODEL), 0.02),
        "norm_mlp_g": 1.0 + nrm(ks[5], (DEPTH, D_MODEL), 0.02),
        "final_norm_g": 1.0 + nrm(ks[6], (D_MODEL,), 0.02),
        "cf_w_pw1": nrm(ks[7], (N_CONF, D_MODEL, 2 * D_MODEL), D_MODEL ** -0.5),
        "cf_b_pw1": nrm(ks[8], (N_CONF, 2 * D_MODEL), 0.02),
        "cf_w_dw": nrm(ks[9], (N_CONF, CONF_KERNEL, D_MODEL), CONF_KERNEL ** -0.5),
        "cf_b_dw": nrm(ks[10], (N_CONF, D_MODEL), 0.02),
        "cf_ln_g": 1.0 + nrm(ks[11], (N_CONF, D_MODEL), 0.02),
        "cf_ln_b": nrm(ks[12], (N_CONF, D_MODEL), 0.02),
        "cf_w_pw2": nrm(ks[13], (N_CONF, D_MODEL, D_MODEL), D_MODEL ** -0.5),
        "cf_b_pw2": nrm(ks[14], (N_CONF, D_MODEL), 0.02),
        "mb_w_in": nrm(ks[15], (N_MAMBA, D_MODEL, MB_IN_COLS), D_MODEL ** -0.5),
        "mb_conv_w": nrm(ks[16], (N_MAMBA, MB_CONV, MB_CONV_CH), MB_CONV ** -0.5),
        "mb_conv_b": nrm(ks[17], (N_MAMBA, MB_CONV_CH), 0.02),
        "mb_dt_bias": dt_bias,
        "mb_a_log": a_log,
        "mb_d": 1.0 + nrm(ks[18], (N_MAMBA, MB_HEADS), 0.02),
        "mb_norm_g": 1.0 + nrm(ks[19], (N_MAMBA, D_INNER), 0.02),
        "mb_w_out": nrm(ks[22], (N_MAMBA, D_INNER, D_MODEL), D_INNER ** -0.5),
        "mlp_w1": nrm(ks[23], (DEPTH, D_MODEL, D_FF), D_MODEL ** -0.5),
        "mlp_w2": nrm(ks[24], (DEPTH, D_FF, D_MODEL), D_FF ** -0.5),
    }


def reference(x, c, w_mod, b_mod, norm_mix_g, norm_mlp_g, final_norm_g,
              cf_w_pw1, cf_b_pw1, cf_w_dw, cf_b_dw, cf_ln_g, cf_ln_b, cf_w_pw2, cf_b_pw2,
              mb_w_in, mb_conv_w, mb_conv_b, mb_dt_bias, mb_a_log, mb_d, mb_norm_g, mb_w_out,
              mlp_w1, mlp_w2):
    h = x
    sc = jax.nn.silu(c)
    for i in range(DEPTH):
        mod = (sc @ w_mod[i] + b_mod[i])[:, None, :]
        sh1, sc1, g1, sh2, sc2, g2 = jnp.split(mod, 6, axis=-1)
        u = rmsnorm(h, norm_mix_g[i]) * (1 + sc1) + sh1
        j = i // N_MIXERS
        if i % N_MIXERS == 0:
            mix = conformer_conv(u, cf_w_pw1[j], cf_b_pw1[j], cf_w_dw[j], cf_b_dw[j],
                                 cf_ln_g[j], cf_ln_b[j], cf_w_pw2[j], cf_b_pw2[j])
        else:
            mix = mamba2(u, mb_w_in[j], mb_conv_w[j], mb_conv_b[j], mb_dt_bias[j],
                         mb_a_log[j], mb_d[j], mb_norm_g[j], mb_w_out[j])
        h = h + g1 * mix
        u = rmsnorm(h, norm_mlp_g[i]) * (1 + sc2) + sh2
        h = h + g2 * sq_relu_mlp(u, mlp_w1[i], mlp_w2[i])
    return rmsnorm(h, final_norm_g)
```

```python
import numpy as np
from contextlib import ExitStack
import concourse.bass as bass
import concourse.mybir as mybir
from concourse.bass_utils import run_bass_kernel_spmd

F32 = mybir.dt.float32
BF16 = mybir.dt.bfloat16
AF = mybir.ActivationFunctionType
ALU = mybir.AluOpType
AX = mybir.AxisListType
EPS = 1e-6
SAME_ENGINE_SYNC = True


class Cfg:
    def __init__(s, D=2048, G=8, T=4096, NQ=4, TT=512, TH=256):
        s.D = D; s.G = G; s.T = T; s.NQ = NQ; s.TT = TT; s.TH = TH
        s.DFF = 4 * D; s.DI = 2 * D; s.NH = s.DI // 64; s.NS = 128
        assert s.NH == 8 * G
        s.KC = D // 128; s.FC = s.DFF // 128; s.IC = s.DI // 128
        s.XC = s.IC + 2 * G
        s.CK = 31; s.HW = 32; s.PRE = 64
        s.GW = G * 512
        o = 0
        def alloc(n):
            nonlocal o
            r = o; o += n; return r
        KC, XC, IC = s.KC, s.XC, s.IC
        s.pv = {}
        for nm, n in [("nmix0", KC), ("nmlp0", KC), ("bpa", KC), ("bpg", KC), ("bdw", KC), ("lng", KC),
                      ("lnb", KC), ("bpw2", KC), ("wdw", KC * 31), ("bmod0", 6 * KC),
                      ("nmix1", KC), ("nmlp1", KC), ("cw", XC * 4), ("cb", XC), ("ng", IC),
                      ("fg", KC), ("bmod1", 6 * KC)]:
            s.pv[nm] = (alloc(n), n)
        s.NV = o
        s.NR = 3 * s.NH


class Buf:
    __slots__ = ("w", "r")

    def __init__(s):
        s.w = None; s.r = {}


class TT_:
    def __init__(s, t):
        s.t = t; s.b = Buf()


class Prog:
    def __init__(s, nc, es):
        s.nc = nc; s.es = es
        s.eng = {"pe": nc.tensor, "act": nc.scalar, "dve": nc.vector, "pool": nc.gpsimd, "sp": nc.sync}
        s.sems = {}
        s.cnt = {}
        for k in s.eng:
            s.sems[k] = es.enter_context(nc.semaphore("sem_" + k)); s.cnt[k] = 0
        s.known = {k: {} for k in s.eng}
        s.nwait = 0; s.nins = 0

    def newsem(s, key):
        s.sems[key] = s.es.enter_context(s.nc.semaphore("sem_" + key)); s.cnt[key] = 0
        return key

    def _deps(s, reads, writes):
        deps = []
        for b in reads:
            if b.w: deps.append(b.w)
        for b in writes:
            if b.w: deps.append(b.w)
            deps.extend(b.r.items())
        return deps

    def _wait(s, e, deps):
        need = {}
        for k, v in deps:
            if k == e:
                if v > s.cnt[e]: continue
                if e in ("pe", "sp") or not SAME_ENGINE_SYNC: continue
            if s.known[e].get(k, 0) >= v: continue
            need[k] = max(need.get(k, 0), v)
        for k, v in need.items():
            s.eng[e].wait_ge(s.sems[k], v); s.known[e][k] = v; s.nwait += 1

    def op(s, e, fn, reads=(), writes=(), sig=True):
        s._wait(e, s._deps(reads, writes))
        ins = fn(s.eng[e]); s.nins += 1
        if sig:
            s.cnt[e] += 1; ins.then_inc(s.sems[e], 1); v = s.cnt[e]
        else:
            v = s.cnt[e] + 1
        for b in reads: b.r[e] = max(b.r.get(e, 0), v)
        for b in writes: b.w = (e, v); b.r = {}

    def dma(s, q, semkey, out, in_, reads=(), writes=()):
        s._wait(q, s._deps(reads, writes))
        ins = s.eng[q].dma_start(out=out, in_=in_); s.nins += 1
        s.cnt[semkey] += 16; ins.then_inc(s.sems[semkey], 16); v = s.cnt[semkey]
        for b in reads: b.r[semkey] = max(b.r.get(semkey, 0), v)
        for b in writes: b.w = (semkey, v); b.r = {}

    def finish(s, e, bufs):
        deps = []
        for b in bufs:
            if b.w: deps.append(b.w)
            deps.extend(b.r.items())
        s._wait(e, deps)


def build(cfg, phase):
    c = cfg
    D, KC, FC, IC, XC, G, NH, T, TT, TH, GW = c.D, c.KC, c.FC, c.IC, c.XC, c.G, c.NH, c.T, c.TT, c.TH, c.GW
    NQ = c.NQ
    nc = bass.Bass("TRN2", target_bir_lowering=False)
    dr = {}

    def din(name, shape, dt=F32):
        dr[name] = nc.dram_tensor(name, list(shape), dt, kind="ExternalInput").ap(); return dr[name]

    def dout(name, shape, dt=F32):
        dr[name] = nc.dram_tensor(name, list(shape), dt, kind="ExternalOutput").ap(); return dr[name]

    din("pvec", [128, c.NV]); din("rowp", [1, c.NR]); din("cvec", [128, KC]); din("hmask", [128, 1])
    din("cstf", [128, 4 * 128])
    if phase == "AB":
        din("x_fm", [D, T]); din("x_halo", [D, c.PRE])
        din("w_mod0", [D, 6 * D]); din("w_mod1", [D, 6 * D])
        din("pw1p", [D, 2 * D]); din("pw2", [D, D]); din("w1_0", [D, c.DFF]); din("w2_0", [c.DFF, D])
        din("w_xbc", [D, XC * 128]); din("w_dt", [D, NH])
        dout("h1", [D, T]); dout("s_loc", [128, GW]); dout("atot", [128, NH]); dout("rawhalo", [128, XC * 3])
    else:
        din("h1", [D, T]); din("rawhalo", [128, XC * 3]); din("s_all", [NQ, 128, GW]); din("atot_all", [128, NQ * NH])
        din("inc", [128, NQ * NQ]); din("valid", [128, NQ])
        din("w_mod1", [D, 6 * D]); din("w_xbc", [D, XC * 128]); din("w_z", [D, c.DI]); din("w_dt", [D, NH])
        din("w_out", [c.DI, D]); din("w1_1", [D, c.DFF]); din("w2_1", [c.DFF, D])
        dout("out_fm", [D, T])

    es = ExitStack()
    with es:
        P = Prog(nc, es)
        blk = es.enter_context(nc.Block())

        def sb(name, shape, dt=F32):
            return TT_(es.enter_context(nc.sbuf_tensor(name, list(shape), dt)))

        h = sb("h", [128, KC, TT]); hB = [Buf() for _ in range(KC)]
        u = sb("u", [128, KC, TT], BF16); uB = [Buf() for _ in range(KC)]
        BIGB = max(FC * TT * 2, 1)
        big = es.enter_context(nc.sbuf_tensor("big", [128, BIGB // 2], BF16))
        NWB = 2
        KBS = min(16, KC)
        wbuf = [sb(f"wb{i}", [128, 16, 512], BF16) for i in range(NWB)]
        wsem = [P.newsem(f"w{i}") for i in range(NWB)]
        pvec = sb("pvec_sb", [128, c.NV]); rowp = sb("rowp_sb", [128, c.NR])
        cstf = sb("cstf_sb", [128, 512]); identb = sb("identb", [128, 128], BF16)
        misc = sb("misc", [128, 8]); hmask = sb("hmask_sb", [128, 1])
        modv = [sb(f"mod{i}", [128, 6 * KC]) for i in range(2)]
        der = sb("der", [128, 6 * KC])
        scv = sb("scv", [128, KC], BF16); cvt = sb("cvt", [128, KC])
        rstd = sb("rstd", [128, TT]); tmpA = [sb(f"tmpA{i}", [128, TT + 4]) for i in range(2)]
        sqs = [sb(f"sqs{i}", [128, TT]) for i in range(2)]
        state = sb("state", [128, GW]); stB = [Buf() for _ in range(G)]
        rawhalo = sb("rawhalo_sb", [128, XC, 3]); rhB = [Buf() for _ in range(XC)]
        rawb = tmpA; accb = sqs
        wdt = sb("wdt", [128, KC, NH], BF16)
        sm = {n: sb("sm_" + n, [128, NH]) for n in ["t1", "t2", "dt", "a", "acs", "eac", "d", "ed", "wend", "etot", "atot"]}
        xs = sb("xs", [128, 640], BF16); xw = sb("xw", [128, 512], BF16)
        ld_sem = P.newsem("ld"); st_sem = P.newsem("st"); c_sem = P.newsem("cld"); cw_sem = P.newsem("cwl"); lds_sem = P.newsem("lds"); strh_sem = P.newsem("strh"); sts_sem = P.newsem("sts"); sta_sem = P.newsem("sta"); c2_sem = P.newsem("cld2")

        ident_f = cstf.t[:, 0:128]; ones_f = cstf.t[:, 128:256]; triu_f = cstf.t[:, 256:384]; ustr_f = cstf.t[:, 384:512]
        epsT = misc.t[:, 0:1]; oneT = misc.t[:, 1:2]

        def pvc(nm, i=0, n=1):
            o, _ = c.pv[nm]
            return pvec.t[:, o + i:o + i + n]

        banks = [TT_(es.enter_context(nc.psum_tensor(f"bank{i}", [128, 512], F32))) for i in range(6)]
        segp = TT_(es.enter_context(nc.psum_tensor("segp", [128, 1024], F32)))
        mm = banks[0:4]; aux0 = banks[4]; aux1 = banks[5]

        P.dma("sp", c_sem, pvec.t[:], dr["pvec"][:], writes=[pvec.b])
        P.dma("sp", c_sem, rowp.t[:], dr["rowp"][0:1, :].partition_broadcast(128), writes=[rowp.b])
        P.dma("sp", c_sem, cstf.t[:], dr["cstf"][:], writes=[cstf.b])
        P.dma("sp", c_sem, cvt.t[:], dr["cvec"][:], writes=[cvt.b])
        P.dma("sp", c_sem, hmask.t[:], dr["hmask"][:], writes=[hmask.b])
        for tt_ in (pvec, rowp, cstf, cvt, hmask):
            tt_.b.w = (c_sem, P.cnt[c_sem])
        P.op("dve", lambda e: e.memset(misc.t[:, 0:1], EPS), writes=[misc.b])
        P.op("dve", lambda e: e.memset(misc.t[:, 1:2], 1.0), writes=[misc.b])
        P.op("dve", lambda e: e.tensor_copy(out=identb.t[:], in_=ident_f), reads=[cstf.b], writes=[identb.b])
        P.op("act", lambda e: e.activation(out=scv.t[:], in_=cvt.t[:], func=AF.Silu), reads=[cvt.b], writes=[scv.b])
        P.op("act", lambda e: e.activation(out=rowp.t[:, NH:2 * NH], in_=rowp.t[:, NH:2 * NH], func=AF.Exp),
             reads=[rowp.b], writes=[rowp.b])
        P.op("dve", lambda e: e.tensor_scalar(out=rowp.t[:, NH:2 * NH], in0=rowp.t[:, NH:2 * NH], scalar1=-1.0,
                                              scalar2=None, op0=ALU.mult), reads=[rowp.b], writes=[rowp.b])
        dtb_bc = rowp.t[:, 0:NH]; aneg_bc = rowp.t[:, NH:2 * NH]; dvec_bc = rowp.t[:, 2 * NH:3 * NH]

        wstate = {"i": 0}

        def wload(src, k0c, nkc, n0, ncols):
            i = wstate["i"] % NWB; wstate["i"] += 1
            wb = wbuf[i]
            P.dma("pool", wsem[i], wb.t[:, 0:nkc, 0:ncols],
                  src[k0c * 128:(k0c + nkc) * 128, n0:n0 + ncols].rearrange("(kc p) n -> p kc n", p=128),
                  writes=[wb.b])
            return wb

        def proj_A(src, K_chunks, ncols_total, rhs_fn, NT, evac, col0=0):
            nkb = (K_chunks + KBS - 1) // KBS
            for nb in range(ncols_total // 512):
                for kb in range(nkb):
                    nkc = min(KBS, K_chunks - kb * KBS)
                    wb = wload(src, kb * KBS, nkc, col0 + nb * 512, 512)
                    for j in range(4):
                        for kc in range(nkc):
                            ap, rb = rhs_fn(kb * KBS + kc)
                            first = (kb == 0 and kc == 0); last = (kb == nkb - 1 and kc == nkc - 1)
                            P.op("pe", lambda e: e.matmul(out=mm[j].t[:, :NT], lhsT=wb.t[:, kc, j * 128:(j + 1) * 128],
                                                          rhs=ap, start=first, stop=last),
                                 reads=[wb.b, rb], writes=[mm[j].b], sig=(last or (j == 3 and kc == nkc - 1)))
                for j in range(4):
                    evac(nb, j, mm[j])

        def compute_mod(layer, wsrc):
            mv = modv[layer]
            for nb in range(6 * D // 512):
                wb = wload(wsrc, 0, KC, nb * 512, 512)
                for j in range(4):
                    for kc in range(KC):
                        P.op("pe", lambda e: e.matmul(out=aux0.t[:, j:j + 1], lhsT=wb.t[:, kc, j * 128:(j + 1) * 128],
                                                      rhs=scv.t[:, kc:kc + 1], start=(kc == 0), stop=(kc == KC - 1)),
                             reads=[wb.b, scv.b], writes=[aux0.b], sig=(kc == KC - 1))
                o, _ = c.pv["bmod%d" % layer]
                P.op("dve", lambda e: e.tensor_tensor(out=mv.t[:, nb * 4:nb * 4 + 4], in0=aux0.t[:, 0:4],
                                                      in1=pvec.t[:, o + nb * 4:o + nb * 4 + 4], op=ALU.add),
                     reads=[aux0.b, pvec.b], writes=[mv.b])

        def derive(layer):
            mv = modv[layer]
            for (dst, nm, sc0) in [(0, "nmix%d" % layer, KC), (KC, "nmlp%d" % layer, 4 * KC)]:
                P.op("dve", lambda e: e.scalar_tensor_tensor(out=der.t[:, dst:dst + KC], in0=mv.t[:, sc0:sc0 + KC], scalar=1.0,
                                                             in1=pvc(nm, 0, KC), op0=ALU.add, op1=ALU.mult),
                     reads=[mv.b, pvec.b], writes=[der.b])
            if layer == 0:
                P.op("dve", lambda e: e.tensor_tensor(out=der.t[:, 2 * KC:3 * KC], in0=mv.t[:, 2 * KC:3 * KC],
                                                      in1=pvc("bpw2", 0, KC), op=ALU.mult),
                     reads=[mv.b, pvec.b], writes=[der.b])

        def rmsnorm_to(NT, src_fn, scale_fn, bias_fn, dst_fn, t0=0):
            for cc in range(KC):
                sq = sqs[cc % 2]
                ap, rb = src_fn(cc)
                P.op("act", lambda e: e.activation(out=sq.t[:, :NT], in_=ap, func=AF.Square), reads=[rb], writes=[sq.b])
                P.op("pe", lambda e: e.matmul(out=aux0.t[:, :NT], lhsT=ones_f, rhs=sq.t[:, :NT], start=(cc == 0),
                                              stop=(cc == KC - 1)), reads=[sq.b, cstf.b], writes=[aux0.b])
            P.op("act", lambda e: e.activation(out=rstd.t[:, :NT], in_=aux0.t[:, :NT], func=AF.Sqrt, scale=1.0 / D, bias=epsT),
                 reads=[aux0.b, misc.b], writes=[rstd.b])
            P.op("dve", lambda e: e.reciprocal(out=rstd.t[:, :NT], in_=rstd.t[:, :NT]), reads=[rstd.b], writes=[rstd.b])
            for cc in range(KC):
                tp = tmpA[cc % 2]
                ap, rb = src_fn(cc)
                P.op("dve", lambda e: e.tensor_tensor(out=tp.t[:, :NT], in0=ap, in1=rstd.t[:, :NT], op=ALU.mult),
                     reads=[rb, rstd.b], writes=[tp.b])
                dap, db = dst_fn(cc)
                sc_ap, sc_b = scale_fn(cc)
                if bias_fn is None:
                    P.op("act", lambda e: e.activation(out=dap, in_=tp.t[:, :NT], func=AF.Copy, scale=sc_ap),
                         reads=[tp.b, sc_b], writes=[db])
                else:
                    bi_ap, bi_b = bias_fn(cc)
                    P.op("act", lambda e: e.activation(out=dap, in_=tp.t[:, :NT], func=AF.Identity, scale=sc_ap, bias=bi_ap),
                         reads=[tp.b, sc_b, bi_b], writes=[db])

        def h_src(NT, t0=0):
            return lambda cc: (h.t[:, cc, t0:t0 + NT], hB[cc])

        def u_dst(NT, t0=0):
            return lambda cc: (u.t[:, cc, t0:t0 + NT], uB[cc])

        def u_rhs(NT, t0=0):
            return lambda kc: (u.t[:, kc, t0:t0 + NT], uB[kc])

        def mlp(NT, layer, w1src, w2src):
            mv = modv[layer]
            rmsnorm_to(NT, h_src(NT), lambda cc: (der.t[:, KC + cc:KC + cc + 1], der.b),
                       lambda cc: (mv.t[:, 3 * KC + cc:3 * KC + cc + 1], mv.b), u_dst(NT))
            hid = big[:, 0:FC * TT].rearrange("p (c t) -> p c t", t=TT)

            def ev1(nb, j, bank):
                fc = nb * 4 + j
                sq = sqs[fc % 2]
                P.op("act", lambda e: e.activation(out=sq.t[:, :NT], in_=bank.t[:, :NT], func=AF.Square),
                     reads=[bank.b], writes=[sq.b])
                P.op("dve", lambda e: e.scalar_tensor_tensor(out=hid[:, fc, :NT], in0=bank.t[:, :NT], scalar=0.0,
                                                             in1=sq.t[:, :NT], op0=ALU.is_gt, op1=ALU.mult),
                     reads=[bank.b, sq.b], writes=[bigB[fc]])
            proj_A(w1src, KC, c.DFF, u_rhs(NT), NT, ev1)

            def ev2(nb, j, bank):
                dc = nb * 4 + j
                P.op("dve", lambda e: e.scalar_tensor_tensor(out=h.t[:, dc, :NT], in0=bank.t[:, :NT],
                                                             scalar=mv.t[:, 5 * KC + dc:5 * KC + dc + 1],
                                                             in1=h.t[:, dc, :NT], op0=ALU.mult, op1=ALU.add),
                     reads=[bank.b, mv.b, hB[dc]], writes=[hB[dc]])
            proj_A(w2src, FC, D, lambda kc: (hid[:, kc, :NT], bigB[kc]), NT, ev2)

        bigB = [Buf() for _ in range(max(FC, 64))]

        XS = TT if phase == "AB" else TH
        xbcT = big[:, 0:XC * XS].rearrange("p (c t) -> p c t", t=XS)
        ZOFF = XC * XS
        NCH_MAX = TT // 128

        def mamba_inproj_xbc(NT, t0, nchunks_xbc):
            def ev(nb, j, bank):
                cc = nb * 4 + j
                rw = rawb[cc % 2]; ac = accb[cc % 2]
                P.op("act", lambda e: e.activation(out=rw.t[:, 0:3], in_=rawhalo.t[:, cc, :], func=AF.Copy),
                     reads=[rhB[cc]], writes=[rw.b])
                P.op("act", lambda e: e.activation(out=rw.t[:, 3:3 + NT], in_=bank.t[:, :NT], func=AF.Copy),
                     reads=[bank.b], writes=[rw.b])
                P.op("act", lambda e: e.activation(out=rawhalo.t[:, cc, :], in_=rw.t[:, NT:NT + 3], func=AF.Copy),
                     reads=[rw.b], writes=[rhB[cc]])
                P.op("dve", lambda e: e.tensor_scalar(out=ac.t[:, :NT], in0=rw.t[:, 0:NT], scalar1=pvc("cw", cc * 4 + 0),
                                                      scalar2=pvc("cb", cc), op0=ALU.mult, op1=ALU.add),
                     reads=[rw.b, pvec.b], writes=[ac.b])
                for k in range(1, 4):
                    P.op("dve", lambda e: e.scalar_tensor_tensor(out=ac.t[:, :NT], in0=rw.t[:, k:k + NT],
                                                                 scalar=pvc("cw", cc * 4 + k), in1=ac.t[:, :NT],
                                                                 op0=ALU.mult, op1=ALU.add),
                         reads=[rw.b, pvec.b, ac.b], writes=[ac.b])
                P.op("act", lambda e: e.activation(out=xbcT[:, cc, :NT], in_=ac.t[:, :NT], func=AF.Silu),
                     reads=[ac.b], writes=[bigB[cc]])
            proj_A(dr["w_xbc"], KC, ((nchunks_xbc + 3) // 4) * 512, u_rhs(NT, t0), NT, ev)

        def ssd_dt(t0, tc):
            for kc in range(KC):
                P.op("pe", lambda e: e.matmul(out=aux0.t[:, 0:NH], lhsT=u.t[:, kc, t0 + tc * 128:t0 + (tc + 1) * 128],
                                              rhs=wdt.t[:, kc, :], start=(kc == 0), stop=(kc == KC - 1)),
                     reads=[uB[kc], wdt.b], writes=[aux0.b], sig=(kc == KC - 1))
            S = sm
            P.op("dve", lambda e: e.tensor_tensor(out=S["t1"].t[:], in0=aux0.t[:, 0:NH], in1=dtb_bc, op=ALU.add),
                 reads=[aux0.b, rowp.b], writes=[S["t1"].b])
            P.op("act", lambda e: e.activation(out=S["t2"].t[:], in_=S["t1"].t[:], func=AF.Exp), reads=[S["t1"].b], writes=[S["t2"].b])
            P.op("act", lambda e: e.activation(out=S["dt"].t[:], in_=S["t2"].t[:], func=AF.Ln, bias=oneT),
                 reads=[S["t2"].b, misc.b], writes=[S["dt"].b])
            P.op("dve", lambda e: e.tensor_tensor(out=S["a"].t[:], in0=S["dt"].t[:], in1=aneg_bc, op=ALU.mult),
                 reads=[S["dt"].b, rowp.b], writes=[S["a"].b])
            P.op("pe", lambda e: e.matmul(out=aux0.t[:, 64:64 + NH], lhsT=triu_f, rhs=S["a"].t[:], start=True, stop=True),
                 reads=[S["a"].b, cstf.b], writes=[aux0.b])
            P.op("pe", lambda e: e.matmul(out=aux0.t[:, 128:128 + NH], lhsT=ones_f, rhs=S["a"].t[:], start=True, stop=True),
                 reads=[S["a"].b, cstf.b], writes=[aux0.b])
            P.op("act", lambda e: e.activation(out=S["acs"].t[:], in_=aux0.t[:, 64:64 + NH], func=AF.Copy),
                 reads=[aux0.b], writes=[S["acs"].b])
            P.op("act", lambda e: e.activation(out=S["eac"].t[:], in_=aux0.t[:, 64:64 + NH], func=AF.Exp),
                 reads=[aux0.b], writes=[S["eac"].b])
            P.op("dve", lambda e: e.tensor_tensor(out=S["d"].t[:], in0=aux0.t[:, 128:128 + NH], in1=S["acs"].t[:], op=ALU.subtract),
                 reads=[aux0.b, S["acs"].b], writes=[S["d"].b])
            P.op("act", lambda e: e.activation(out=S["ed"].t[:], in_=S["d"].t[:], func=AF.Exp), reads=[S["d"].b], writes=[S["ed"].b])
            P.op("dve", lambda e: e.tensor_tensor(out=S["wend"].t[:], in0=S["ed"].t[:], in1=S["dt"].t[:], op=ALU.mult),
                 reads=[S["ed"].b, S["dt"].b], writes=[S["wend"].b])
            P.op("act", lambda e: e.activation(out=S["etot"].t[:], in_=aux0.t[:, 128:128 + NH], func=AF.Exp),
                 reads=[aux0.b], writes=[S["etot"].b])
            P.op("dve", lambda e: e.tensor_tensor(out=S["atot"].t[:], in0=S["atot"].t[:], in1=aux0.t[:, 128:128 + NH], op=ALU.add),
                 reads=[aux0.b, S["atot"].b], writes=[S["atot"].b])

        aux1b = aux1.t[:].bitcast(BF16)

        def ssd_tokmajor(g, tc):
            for j in range(4):
                P.op("pe", lambda e: e.transpose(out=aux1b[:, j * 128:(j + 1) * 128],
                                                 in_=xbcT[:, g * 4 + j, tc * 128:(tc + 1) * 128], identity=identb.t[:]),
                     reads=[bigB[g * 4 + j], identb.b], writes=[aux1.b], sig=False)
            P.op("pe", lambda e: e.transpose(out=aux1b[:, 512:640], in_=xbcT[:, IC + g, tc * 128:(tc + 1) * 128],
                                             identity=identb.t[:]), reads=[bigB[IC + g], identb.b], writes=[aux1.b])
            P.op("act", lambda e: e.activation(out=xs.t[:, 0:640], in_=aux1b[:, 0:640], func=AF.Copy),
                 reads=[aux1.b], writes=[xs.b])

        def bc8(ap, n):
            return ap.unsqueeze(2).to_broadcast([128, 8, n])

        def ssd_state_update(g):
            S = sm
            P.op("dve", lambda e: e.tensor_tensor(out=xw.t[:].rearrange("p (h q) -> p h q", q=64),
                                                  in0=xs.t[:, 0:512].rearrange("p (h q) -> p h q", q=64),
                                                  in1=bc8(S["wend"].t[:, g * 8:(g + 1) * 8], 64), op=ALU.mult),
                 reads=[xs.b, S["wend"].b], writes=[xw.b])
            P.op("pe", lambda e: e.matmul(out=mm[2].t[:, :], lhsT=xs.t[:, 512:640], rhs=xw.t[:], start=True, stop=True),
                 reads=[xs.b, xw.b], writes=[mm[2].b])
            stg = state.t[:, g * 512:(g + 1) * 512]
            P.op("dve", lambda e: e.tensor_tensor(out=stg.rearrange("p (h q) -> p h q", q=64),
                                                  in0=stg.rearrange("p (h q) -> p h q", q=64),
                                                  in1=bc8(S["etot"].t[:, g * 8:(g + 1) * 8], 64), op=ALU.mult),
                 reads=[stB[g], S["etot"].b], writes=[stB[g]])
            P.op("dve", lambda e: e.tensor_tensor(out=stg, in0=stg, in1=mm[2].t[:, :], op=ALU.add),
                 reads=[stB[g], mm[2].b], writes=[stB[g]])

        if phase == "AB":
            vpad = big[:, 0:KC * (c.HW + TT)].rearrange("p (c t) -> p c t", t=c.HW + TT)
            VB = [Buf() for _ in range(KC)]
            CVO_OFF = KC * (c.HW + TT)
            CVO_OFF += CVO_OFF % 2
            cvo = big[:, CVO_OFF:CVO_OFF + 2 * KC * TT].bitcast(F32).rearrange("p (c t) -> p c t", t=TT)
            CB = [Buf() for _ in range(KC)]
            dgs = [sb(f"dg{i}", [128, 31, 128], BF16) for i in range(1)]
            mean_t = sb("mean_t", [128, TT]); msq = sb("msq", [128, TT])
            P.dma("pool", cw_sem, wdt.t[:], dr["w_dt"].rearrange("(kc p) n -> p kc n", p=128), writes=[wdt.b])
            compute_mod(0, dr["w_mod0"]); compute_mod(1, dr["w_mod1"])
            P.op("dve", lambda e: e.memset(state.t[:], 0.0), writes=stB)
            P.op("dve", lambda e: e.memset(sm["atot"].t[:], 0.0), writes=[sm["atot"].b])
            P.op("dve", lambda e: e.memset(rawhalo.t[:], 0.0), writes=rhB)
            P.op("dve", lambda e: e.memset(vpad[:, :, 0:c.HW], 0.0), writes=VB)

            def layer0(NT):
                mv = modv[0]
                rmsnorm_to(NT, h_src(NT), lambda cc: (der.t[:, cc:cc + 1], der.b),
                           lambda cc: (mv.t[:, cc:cc + 1], mv.b), u_dst(NT))

                def evg(nb, j, bank):
                    if j < 2: return
                    jj = j - 2; cc = nb * 2 + jj
                    sg = tmpA[cc % 2]
                    P.op("act", lambda e: e.activation(out=sg.t[:, :NT], in_=bank.t[:, :NT], func=AF.Sigmoid, bias=pvc("bpg", cc)),
                         reads=[bank.b, pvec.b], writes=[sg.b])
                    P.op("dve", lambda e: e.scalar_tensor_tensor(out=vpad[:, cc, c.HW:c.HW + NT], in0=mm[jj].t[:, :NT],
                                                                 scalar=pvc("bpa", cc), in1=sg.t[:, :NT],
                                                                 op0=ALU.add, op1=ALU.mult),
                         reads=[mm[jj].b, pvec.b, sg.b], writes=[VB[cc]])
                proj_A(dr["pw1p"], KC, 2 * D, u_rhs(NT), NT, evg)

                for cc in range(KC):
                    dg = dgs[0]
                    o, _ = c.pv["wdw"]
                    P.op("dve", lambda e: e.tensor_tensor(out=dg.t[:], in0=ident_f.unsqueeze(1).to_broadcast([128, 31, 128]),
                                                          in1=pvec.t[:, o + cc * 31:o + (cc + 1) * 31].unsqueeze(2).to_broadcast([128, 31, 128]),
                                                          op=ALU.mult), reads=[cstf.b, pvec.b], writes=[dg.b])
                    bank = mm[cc % 4]
                    for k in range(31):
                        P.op("pe", lambda e: e.matmul(out=bank.t[:, :NT], lhsT=dg.t[:, k, :],
                                                      rhs=vpad[:, cc, c.HW - 30 + k:c.HW - 30 + k + NT], start=(k == 0), stop=(k == 30)),
                             reads=[dg.b, VB[cc]], writes=[bank.b], sig=(k == 30))
                    sq = sqs[cc % 2]
                    P.op("act", lambda e: e.activation(out=cvo[:, cc, :NT], in_=bank.t[:, :NT], func=AF.Identity, bias=pvc("bdw", cc)),
                         reads=[bank.b, pvec.b], writes=[CB[cc]])
                    P.op("act", lambda e: e.activation(out=sq.t[:, :NT], in_=bank.t[:, :NT], func=AF.Square, bias=pvc("bdw", cc)),
                         reads=[bank.b, pvec.b], writes=[sq.b])
                    P.op("pe", lambda e: e.matmul(out=aux0.t[:, :NT], lhsT=ones_f, rhs=cvo[:, cc, :NT], start=(cc == 0), stop=(cc == KC - 1)),
                         reads=[CB[cc], cstf.b], writes=[aux0.b])
                    P.op("pe", lambda e: e.matmul(out=aux1.t[:, :NT], lhsT=ones_f, rhs=sq.t[:, :NT], start=(cc == 0), stop=(cc == KC - 1)),
                         reads=[sq.b, cstf.b], writes=[aux1.b])
                P.op("dve", lambda e: e.tensor_scalar(out=mean_t.t[:, :NT], in0=aux0.t[:, :NT], scalar1=1.0 / D, scalar2=None, op0=ALU.mult),
                     reads=[aux0.b], writes=[mean_t.b])
                P.op("dve", lambda e: e.tensor_tensor(out=msq.t[:, :NT], in0=mean_t.t[:, :NT], in1=mean_t.t[:, :NT], op=ALU.mult),
                     reads=[mean_t.b], writes=[msq.b])
                P.op("dve", lambda e: e.scalar_tensor_tensor(out=msq.t[:, :NT], in0=aux1.t[:, :NT], scalar=1.0 / D, in1=msq.t[:, :NT],
                                                             op0=ALU.mult, op1=ALU.subtract), reads=[aux1.b, msq.b], writes=[msq.b])
                P.op("act", lambda e: e.activation(out=rstd.t[:, :NT], in_=msq.t[:, :NT], func=AF.Sqrt, bias=epsT),
                     reads=[msq.b, misc.b], writes=[rstd.b])
                P.op("dve", lambda e: e.reciprocal(out=rstd.t[:, :NT], in_=rstd.t[:, :NT]), reads=[rstd.b], writes=[rstd.b])
                for cc in range(KC):
                    tp = tmpA[cc % 2]
                    P.op("dve", lambda e: e.tensor_tensor(out=tp.t[:, :NT], in0=cvo[:, cc, :NT], in1=mean_t.t[:, :NT], op=ALU.subtract),
                         reads=[CB[cc], mean_t.b], writes=[tp.b])
                    P.op("dve", lambda e: e.tensor_tensor(out=tp.t[:, :NT], in0=tp.t[:, :NT], in1=rstd.t[:, :NT], op=ALU.mult),
                         reads=[tp.b, rstd.b], writes=[tp.b])
                    P.op("act", lambda e: e.activation(out=u.t[:, cc, :NT], in_=tp.t[:, :NT], func=AF.Silu, scale=pvc("lng", cc), bias=pvc("lnb", cc)),
                         reads=[tp.b, pvec.b], writes=[uB[cc]])

                def ev2(nb, j, bank):
                    dc = nb * 4 + j
                    tp = tmpA[dc % 2]
                    P.op("act", lambda e: e.activation(out=tp.t[:, :NT], in_=bank.t[:, :NT], func=AF.Identity,
                                                       scale=mv.t[:, 2 * KC + dc:2 * KC + dc + 1], bias=der.t[:, 2 * KC + dc:2 * KC + dc + 1]),
                         reads=[bank.b, mv.b, der.b], writes=[tp.b])
                    P.op("dve", lambda e: e.tensor_tensor(out=h.t[:, dc, :NT], in0=h.t[:, dc, :NT], in1=tp.t[:, :NT], op=ALU.add),
                         reads=[tp.b, hB[dc]], writes=[hB[dc]])
                proj_A(dr["pw2"], KC, D, u_rhs(NT), NT, ev2)

            def save_vhalo(NT, masked):
                for cc in range(KC):
                    if masked:
                        P.op("dve", lambda e: e.tensor_scalar(out=vpad[:, cc, 0:c.HW], in0=vpad[:, cc, NT:NT + c.HW], scalar1=hmask.t[:, 0:1],
                                                              scalar2=None, op0=ALU.mult), reads=[VB[cc], hmask.b], writes=[VB[cc]])
                    else:
                        P.op("dve", lambda e: e.tensor_copy(out=vpad[:, cc, 0:c.HW], in_=vpad[:, cc, NT:NT + c.HW]),
                             reads=[VB[cc]], writes=[VB[cc]])

            vh = sb("vh", [128, KC, c.HW], BF16)

            tiles = [("pre", 0, c.PRE)] + [("main", i * TT, TT) for i in range(T // TT)]
            for (kind, t0, NT) in tiles:
                src = dr["x_halo"] if kind == "pre" else dr["x_fm"]
                P.dma("sp", ld_sem, h.t[:, :, 0:NT], src[:, t0:t0 + NT].rearrange("(c p) t -> p c t", p=128), writes=hB)
                derive(0)
                if kind == "pre":
                    P.op("dve", lambda e: e.memset(vpad[:, :, 0:c.HW], 0.0), writes=VB)
                else:
                    P.op("dve", lambda e: e.tensor_copy(out=vpad[:, :, 0:c.HW], in_=vh.t[:]), reads=[vh.b], writes=VB)
                layer0(NT)
                if kind == "pre":
                    P.op("dve", lambda e: e.tensor_scalar(out=vh.t[:], in0=vpad[:, :, NT:NT + c.HW], scalar1=hmask.t[:, 0:1], scalar2=None,
                                                          op0=ALU.mult), reads=VB + [hmask.b], writes=[vh.b])
                else:
                    P.op("dve", lambda e: e.tensor_copy(out=vh.t[:], in_=vpad[:, :, NT:NT + c.HW]), reads=VB, writes=[vh.b])
                mlp(NT, 0, dr["w1_0"], dr["w2_0"])
                if kind == "main":
                    P.dma("sp", st_sem, dr["h1"][:, t0:t0 + NT].rearrange("(c p) t -> p c t", p=128), h.t[:, :, 0:NT], reads=hB)
                derive(1)
                mv = modv[1]
                rmsnorm_to(NT, h_src(NT), lambda cc: (der.t[:, cc:cc + 1], der.b), lambda cc: (mv.t[:, cc:cc + 1], mv.b), u_dst(NT))
                if kind == "pre":
                    mamba_inproj_xbc(NT, 0, XC)
                    for cc in range(XC):
                        P.op("dve", lambda e: e.tensor_scalar(out=rawhalo.t[:, cc, :], in0=rawhalo.t[:, cc, :], scalar1=hmask.t[:, 0:1],
                                                              scalar2=None, op0=ALU.mult), reads=[rhB[cc], hmask.b], writes=[rhB[cc]])
                    P.dma("sp", strh_sem, dr["rawhalo"][:], rawhalo.t[:].rearrange("p c k -> p (c k)"), reads=rhB)
                else:
                    mamba_inproj_xbc(NT, 0, IC + G)
                    for tc in range(NT // 128):
                        ssd_dt(0, tc)
                        for g in range(G):
                            ssd_tokmajor(g, tc)
                            ssd_state_update(g)
            P.dma("sp", sts_sem, dr["s_loc"][:], state.t[:], reads=stB)
            P.dma("sp", sta_sem, dr["atot"][:], sm["atot"].t[:], reads=[sm["atot"].b])
            P.finish("sp", hB + stB + [sm["atot"].b] + rhB)
            P.eng["sp"].wait_ge(P.sems[st_sem], P.cnt[st_sem])

        else:
            NCH = TH // 128
            zs = big[:, ZOFF:ZOFF + NCH * c.DI].rearrange("p (t c) -> p t c", c=c.DI)
            ZB = [Buf() for _ in range(NCH)]
            YOFF = ZOFF + NCH * c.DI
            assert YOFF + IC * TH <= BIGB // 2
            ynT = TT_(big[:, YOFF:YOFF + IC * TH].rearrange("p (c t) -> p c t", t=TH)); YB = [Buf() for _ in range(IC)]
            stbf = sb("stbf", [128, 512], BF16)
            Lb = sb("Lb", [128, 8, 128]); dec = sb("dec", [128, 8, 128]); sc1 = dec
            scT = sb("scT", [128, 8, 128], BF16); cbm = sb("cbm", [128, 128])
            yb = sb("yb", [128, 512]); yt = sb("yt", [128, 512]); ssq = sb("ssq", [128, 2])
            coef = sb("coef", [128, NH]); cin = sb("cin", [128, NQ * NQ + NQ]); atl = sb("atl", [128, NQ * NH])
            stmp = TT_(big[:, 0:2 * GW].bitcast(F32))
            P.dma("pool", cw_sem, wdt.t[:], dr["w_dt"].rearrange("(kc p) n -> p kc n", p=128), writes=[wdt.b])
            P.dma("sp", c2_sem, cin.t[:, 0:NQ * NQ], dr["inc"][:], writes=[cin.b])
            P.dma("sp", c2_sem, cin.t[:, NQ * NQ:], dr["valid"][:], writes=[cin.b])
            P.dma("sp", c2_sem, atl.t[:], dr["atot_all"][:], writes=[atl.b])
            P.dma("sp", c2_sem, rawhalo.t[:].rearrange("p c k -> p (c k)"), dr["rawhalo"][:], writes=rhB)
            for b_ in [cin.b, atl.b] + rhB:
                b_.w = (c2_sem, P.cnt[c2_sem])
            compute_mod(1, dr["w_mod1"])
            derive(1)
            P.op("dve", lambda e: e.memset(sm["atot"].t[:], 0.0), writes=[sm["atot"].b])
            P.op("dve", lambda e: e.memset(state.t[:], 0.0), writes=stB)
            for j in range(NQ):
                P.op("dve", lambda e: e.tensor_scalar(out=coef.t[:], in0=atl.t[:, 0:NH], scalar1=cin.t[:, j * NQ:j * NQ + 1], scalar2=None,
                                                      op0=ALU.mult), reads=[atl.b, cin.b], writes=[coef.b])
                for m in range(1, NQ):
                    P.op("dve", lambda e: e.scalar_tensor_tensor(out=coef.t[:], in0=atl.t[:, m * NH:(m + 1) * NH],
                                                                 scalar=cin.t[:, j * NQ + m:j * NQ + m + 1], in1=coef.t[:],
                                                                 op0=ALU.mult, op1=ALU.add), reads=[atl.b, cin.b, coef.b], writes=[coef.b])
                P.op("act", lambda e: e.activation(out=coef.t[:], in_=coef.t[:], func=AF.Exp), reads=[coef.b], writes=[coef.b])
                P.op("dve", lambda e: e.tensor_scalar(out=coef.t[:], in0=coef.t[:], scalar1=cin.t[:, NQ * NQ + j:NQ * NQ + j + 1], scalar2=None,
                                                      op0=ALU.mult), reads=[coef.b, cin.b], writes=[coef.b])
                P.dma("sp", lds_sem, stmp.t[:], dr["s_all"][j], writes=[stmp.b])
                P.op("dve", lambda e: e.tensor_tensor(out=stmp.t[:].rearrange("p (h q) -> p h q", q=64),
                                                      in0=stmp.t[:].rearrange("p (h q) -> p h q", q=64),
                                                      in1=coef.t[:].unsqueeze(2).to_broadcast([128, NH, 64]), op=ALU.mult),
                     reads=[stmp.b, coef.b], writes=[stmp.b])
                P.op("dve", lambda e: e.tensor_tensor(out=state.t[:], in0=state.t[:], in1=stmp.t[:], op=ALU.add),
                     reads=[stmp.b] + stB, writes=stB)

            def mamba_full(t0, NT):
                mamba_inproj_xbc(NT, t0, XC)
                for nb in range(c.DI // 512):
                    wb = wload(dr["w_z"], 0, KC, nb * 512, 512)
                    for tc in range(NT // 128):
                        for kc in range(KC):
                            P.op("pe", lambda e: e.matmul(out=mm[tc].t[:, :], lhsT=u.t[:, kc, t0 + tc * 128:t0 + (tc + 1) * 128],
                                                          rhs=wb.t[:, kc, :], start=(kc == 0), stop=(kc == KC - 1)),
                                 reads=[uB[kc], wb.b], writes=[mm[tc].b], sig=(kc == KC - 1))
                        P.op("act", lambda e: e.activation(out=zs[:, tc, nb * 512:(nb + 1) * 512], in_=mm[tc].t[:, :], func=AF.Silu),
                             reads=[mm[tc].b], writes=[ZB[tc]])
                S = sm
                for tc in range(NT // 128):
                    ssd_dt(t0, tc)
                    sl = slice(tc * 128, (tc + 1) * 128)
                    for g in range(G):
                        g8 = slice(g * 8, (g + 1) * 8)
                        ssd_tokmajor(g, tc)
                        P.op("act", lambda e: e.activation(out=stbf.t[:], in_=state.t[:, g * 512:(g + 1) * 512], func=AF.Copy),
                             reads=[stB[g]], writes=[stbf.b])
                        P.op("pe", lambda e: e.matmul(out=mm[1].t[:, :], lhsT=xbcT[:, IC + G + g, sl], rhs=stbf.t[:], start=True, stop=True),
                             reads=[bigB[IC + G + g], stbf.b], writes=[mm[1].b])
                        P.op("pe", lambda e: e.matmul(out=aux1.t[:, 384:512], lhsT=xbcT[:, IC + g, sl], rhs=xbcT[:, IC + G + g, sl],
                                                      start=True, stop=True), reads=[bigB[IC + g], bigB[IC + G + g]], writes=[aux1.b])
                        P.op("dve", lambda e: e.tensor_tensor(out=cbm.t[:], in0=aux1.t[:, 384:512], in1=triu_f, op=ALU.mult),
                             reads=[aux1.b, cstf.b], writes=[cbm.b])
                        P.op("dve", lambda e: e.tensor_tensor(out=Lb.t[:], in0=ustr_f.unsqueeze(1).to_broadcast([128, 8, 128]),
                                                              in1=bc8(S["a"].t[:, g8], 128), op=ALU.mult),
                             reads=[cstf.b, S["a"].b], writes=[Lb.b])
                        for hh in range(8):
                            P.op("pe", lambda e: e.matmul(out=segp.t[:, hh * 128:(hh + 1) * 128], lhsT=Lb.t[:, hh, :], rhs=triu_f,
                                                          start=True, stop=True), reads=[Lb.b, cstf.b], writes=[segp.b], sig=(hh == 7))
                        P.op("act", lambda e: e.activation(out=dec.t[:, 0:4, :], in_=segp.t[:, 0:512].rearrange("p (h l) -> p h l", l=128), func=AF.Exp),
                             reads=[segp.b], writes=[dec.b])
                        P.op("act", lambda e: e.activation(out=dec.t[:, 4:8, :], in_=segp.t[:, 512:1024].rearrange("p (h l) -> p h l", l=128), func=AF.Exp),
                             reads=[segp.b], writes=[dec.b])
                        P.op("dve", lambda e: e.tensor_tensor(out=sc1.t[:], in0=dec.t[:], in1=bc8(S["dt"].t[:, g8], 128), op=ALU.mult),
                             reads=[dec.b, S["dt"].b], writes=[sc1.b])
                        P.op("dve", lambda e: e.tensor_tensor(out=scT.t[:], in0=sc1.t[:], in1=cbm.t[:].unsqueeze(1).to_broadcast([128, 8, 128]), op=ALU.mult),
                             reads=[sc1.b, cbm.b], writes=[scT.b])
                        for hh in range(8):
                            P.op("pe", lambda e: e.matmul(out=mm[0].t[:, hh * 64:(hh + 1) * 64], lhsT=scT.t[:, hh, :], rhs=xs.t[:, hh * 64:(hh + 1) * 64],
                                                          start=True, stop=True), reads=[scT.b, xs.b], writes=[mm[0].b], sig=(hh == 7))
                        P.op("dve", lambda e: e.tensor_tensor(out=yb.t[:].rearrange("p (h q) -> p h q", q=64),
                                                              in0=mm[1].t[:, :].rearrange("p (h q) -> p h q", q=64),
                                                              in1=bc8(S["eac"].t[:, g8], 64), op=ALU.mult),
                             reads=[mm[1].b, S["eac"].b], writes=[yb.b])
                        P.op("dve", lambda e: e.tensor_tensor(out=yb.t[:], in0=yb.t[:], in1=mm[0].t[:, :], op=ALU.add),
                             reads=[yb.b, mm[0].b], writes=[yb.b])
                        P.op("dve", lambda e: e.tensor_tensor(out=yt.t[:].rearrange("p (h q) -> p h q", q=64), in0=xs.t[:, 0:512].rearrange("p (h q) -> p h q", q=64), in1=bc8(dvec_bc[:, g8], 64), op=ALU.mult),
                             reads=[xs.b, rowp.b], writes=[yt.b])
                        P.op("dve", lambda e: e.tensor_tensor(out=yb.t[:], in0=yb.t[:], in1=yt.t[:], op=ALU.add),
                             reads=[yb.b, yt.b], writes=[yb.b])
                        P.op("dve", lambda e: e.tensor_tensor(out=yb.t[:], in0=yb.t[:], in1=zs[:, tc, g * 512:(g + 1) * 512], op=ALU.mult),
                             reads=[yb.b, ZB[tc]], writes=[yb.b])
                        P.op("dve", lambda e: e.tensor_tensor(out=yt.t[:], in0=yb.t[:], in1=yb.t[:], op=ALU.mult), reads=[yb.b], writes=[yt.b])
                        P.op("dve", lambda e: e.tensor_reduce(out=ssq.t[:, 0:1], in_=yt.t[:], axis=AX.X, op=ALU.add), reads=[yt.b], writes=[ssq.b])
                        P.op("act", lambda e: e.activation(out=ssq.t[:, 1:2], in_=ssq.t[:, 0:1], func=AF.Sqrt, scale=1.0 / 512, bias=epsT),
                             reads=[ssq.b, misc.b], writes=[ssq.b])
                        P.op("dve", lambda e: e.reciprocal(out=ssq.t[:, 1:2], in_=ssq.t[:, 1:2]), reads=[ssq.b], writes=[ssq.b])
                        P.op("dve", lambda e: e.tensor_scalar(out=yb.t[:], in0=yb.t[:], scalar1=ssq.t[:, 1:2], scalar2=None, op0=ALU.mult),
                             reads=[yb.b, ssq.b], writes=[yb.b])
                        for j in range(4):
                            P.op("pe", lambda e: e.transpose(out=mm[3].t[:, j * 128:(j + 1) * 128], in_=yb.t[:, j * 128:(j + 1) * 128], identity=ident_f),
                                 reads=[yb.b, cstf.b], writes=[mm[3].b], sig=(j == 3))
                        for j in range(4):
                            P.op("act", lambda e: e.activation(out=ynT.t[:, g * 4 + j, sl], in_=mm[3].t[:, j * 128:(j + 1) * 128], func=AF.Copy,
                                                               scale=pvc("ng", g * 4 + j)), reads=[mm[3].b, pvec.b], writes=[YB[g * 4 + j]])
                        ssd_state_update(g)
                mv = modv[1]

                def evo(nb, j, bank):
                    dc = nb * 4 + j
                    P.op("dve", lambda e: e.scalar_tensor_tensor(out=h.t[:, dc, t0:t0 + NT], in0=bank.t[:, :NT],
                                                                 scalar=mv.t[:, 2 * KC + dc:2 * KC + dc + 1], in1=h.t[:, dc, t0:t0 + NT],
                                                                 op0=ALU.mult, op1=ALU.add), reads=[bank.b, mv.b, hB[dc]], writes=[hB[dc]])
                proj_A(dr["w_out"], IC, D, lambda kc: (ynT.t[:, kc, :NT], YB[kc]), NT, evo)

            mv = modv[1]
            obuf = tmpA
            for i in range(T // TT):
                t0 = i * TT
                P.dma("sp", ld_sem, h.t[:], dr["h1"][:, t0:t0 + TT].rearrange("(c p) t -> p c t", p=128), writes=hB)
                rmsnorm_to(TT, h_src(TT), lambda cc: (der.t[:, cc:cc + 1], der.b), lambda cc: (mv.t[:, cc:cc + 1], mv.b), u_dst(TT))
                for hf in range(TT // TH):
                    mamba_full(hf * TH, TH)
                mlp(TT, 1, dr["w1_1"], dr["w2_1"])
                ost = big[:, 0:2 * KC * TT].bitcast(F32).rearrange("p (c t) -> p c t", t=TT)
                rmsnorm_to(TT, h_src(TT), lambda cc: (pvc("fg", cc), pvec.b), None, lambda cc: (ost[:, cc, :], bigB[cc]))
                P.dma("sp", st_sem, dr["out_fm"][:, t0:t0 + TT].rearrange("(c p) t -> p c t", p=128), ost, reads=bigB)
            P.finish("sp", bigB)
            P.eng["sp"].wait_ge(P.sems[st_sem], P.cnt[st_sem])
    nc._prog_stats = (P.nins, P.nwait)
    return nc


def _fm(v, n=None):
    v = np.asarray(v, np.float32)
    return np.ascontiguousarray(v.reshape(-1, 128).T)


def _consts():
    i = np.arange(128)
    ident = np.eye(128, dtype=np.float32)
    ones = np.ones((128, 128), np.float32)
    triu = (i[:, None] <= i[None, :]).astype(np.float32)
    ustr = (i[:, None] > i[None, :]).astype(np.float32)
    return np.ascontiguousarray(np.concatenate([ident, ones, triu, ustr], axis=1))


def host_prep(cfg, inp, n_cores):
    c = cfg
    D, KC, XC, IC, G, NH, T = c.D, c.KC, c.XC, c.IC, c.G, c.NH, c.T
    f = lambda a: np.asarray(a, np.float32)
    pv = np.zeros((128, c.NV), np.float32)

    def put(nm, arr):
        o, n = c.pv[nm]
        assert arr.shape == (128, n), (nm, arr.shape, n)
        pv[:, o:o + n] = arr
    put("nmix0", _fm(inp["norm_mix_g"][0])); put("nmlp0", _fm(inp["norm_mlp_g"][0]))
    b1 = f(inp["cf_b_pw1"][0]); put("bpa", _fm(b1[:D])); put("bpg", _fm(b1[D:]))
    put("bdw", _fm(inp["cf_b_dw"][0])); put("lng", _fm(inp["cf_ln_g"][0])); put("lnb", _fm(inp["cf_ln_b"][0]))
    put("bpw2", _fm(inp["cf_b_pw2"][0]))
    wdw = f(inp["cf_w_dw"][0])
    put("wdw", np.ascontiguousarray(wdw.T.reshape(KC, 128, 31).transpose(1, 0, 2).reshape(128, KC * 31)))
    put("bmod0", _fm(inp["b_mod"][0])); put("bmod1", _fm(inp["b_mod"][1]))
    put("nmix1", _fm(inp["norm_mix_g"][1])); put("nmlp1", _fm(inp["norm_mlp_g"][1]))
    cw = f(inp["mb_conv_w"][0])
    put("cw", np.ascontiguousarray(cw.T.reshape(XC, 128, 4).transpose(1, 0, 2).reshape(128, XC * 4)))
    put("cb", _fm(inp["mb_conv_b"][0])); put("ng", _fm(inp["mb_norm_g"][0])); put("fg", _fm(inp["final_norm_g"]))
    rowp = np.concatenate([f(inp["mb_dt_bias"][0]), f(inp["mb_a_log"][0]), f(inp["mb_d"][0])])[None, :]
    rowp = np.ascontiguousarray(rowp)
    w1 = f(inp["cf_w_pw1"][0])
    nb = 2 * D // 512
    pw1p = np.ascontiguousarray(np.concatenate(
        [np.concatenate([w1[:, 256 * i:256 * i + 256], w1[:, D + 256 * i:D + 256 * i + 256]], axis=1) for i in range(nb)], axis=1))
    w_in = f(inp["mb_w_in"][0])
    DI = c.DI
    w_z = np.ascontiguousarray(w_in[:, :DI]); w_xbc = np.ascontiguousarray(w_in[:, DI:DI + XC * 128])
    w_dt = np.ascontiguousarray(w_in[:, DI + XC * 128:])
    shared = dict(pvec=pv, rowp=rowp, cstf=_consts())
    wAB = dict(w_mod0=f(inp["w_mod"][0]), w_mod1=f(inp["w_mod"][1]), pw1p=pw1p, pw2=f(inp["cf_w_pw2"][0]),
               w1_0=f(inp["mlp_w1"][0]), w2_0=f(inp["mlp_w2"][0]), w_xbc=w_xbc, w_dt=w_dt)
    wC = dict(w_mod1=f(inp["w_mod"][1]), w_xbc=w_xbc, w_z=w_z, w_dt=w_dt, w_out=f(inp["mb_w_out"][0]),
              w1_1=f(inp["mlp_w1"][1]), w2_1=f(inp["mlp_w2"][1]))
    x = f(inp["x"]); B, S, _ = x.shape
    per_seq = S // T
    assert per_seq == c.NQ and B * per_seq == n_cores
    cores = []
    for k in range(n_cores):
        b, r = divmod(k, per_seq)
        xs_ = x[b, r * T:(r + 1) * T, :]
        halo = np.zeros((c.PRE, D), np.float32)
        if r > 0:
            halo[:] = x[b, r * T - c.PRE:r * T, :]
        cores.append(dict(b=b, r=r, x_fm=np.ascontiguousarray(xs_.T), x_halo=np.ascontiguousarray(halo.T),
                          cvec=_fm(inp["c"][b]), hmask=np.full((128, 1), 1.0 if r > 0 else 0.0, np.float32)))
    return shared, wAB, wC, cores


_NC_CACHE = {}


def _get_nc(cfg, phase):
    key = (cfg.D, cfg.G, cfg.T, cfg.NQ, phase)
    if key not in _NC_CACHE:
        _NC_CACHE[key] = build(cfg, phase)
    return _NC_CACHE[key]


def run_module(cfg, inp, n_cores):
    c = cfg
    shared, wAB, wC, cores = host_prep(cfg, inp, n_cores)
    ncA = build(cfg, "AB")
    mapsA = []
    for k in range(n_cores):
        m = dict(shared); m.update(wAB)
        m.update(x_fm=cores[k]["x_fm"], x_halo=cores[k]["x_halo"], cvec=cores[k]["cvec"], hmask=cores[k]["hmask"])
        mapsA.append(m)
    resA = run_bass_kernel_spmd(ncA, mapsA, core_ids=list(range(n_cores))).results
    ncC = build(cfg, "C")
    NQ = c.NQ
    mapsC = []
    for k in range(n_cores):
        b, r = cores[k]["b"], cores[k]["r"]
        grp = [b * NQ + j for j in range(NQ)]
        s_all = np.ascontiguousarray(np.stack([resA[j]["s_loc"] for j in grp], axis=0))
        atot_all = np.ascontiguousarray(np.concatenate([resA[j]["atot"] for j in grp], axis=1))
        inc = np.zeros((NQ, NQ), np.float32); valid = np.zeros((NQ,), np.float32)
        for j in range(NQ):
            valid[j] = 1.0 if j < r else 0.0
            for m_ in range(NQ):
                inc[j, m_] = 1.0 if (j < m_ < r) else 0.0
        m = dict(shared); m.update(wC)
        m.update(h1=resA[k]["h1"], rawhalo=resA[k]["rawhalo"], s_all=s_all, atot_all=atot_all,
                 inc=np.ascontiguousarray(np.broadcast_to(inc.reshape(1, -1), (128, NQ * NQ))),
                 valid=np.ascontiguousarray(np.broadcast_to(valid.reshape(1, -1), (128, NQ))),
                 cvec=cores[k]["cvec"], hmask=cores[k]["hmask"])
        mapsC.append(m)
    resC = run_bass_kernel_spmd(ncC, mapsC, core_ids=list(range(n_cores))).results
    B = n_cores // NQ
    out = np.empty((B, NQ * c.T, c.D), np.float32)
    for k in range(n_cores):
        b, r = cores[k]["b"], cores[k]["r"]
        out[b, r * c.T:(r + 1) * c.T, :] = resC[k]["out_fm"].T
    return out, resA, resC


def kernel(**inputs):
    cfg = Cfg()
    out, _, _ = run_module(cfg, inputs, 8)
    return out
```

```python
import numpy as np
from contextlib import ExitStack
import concourse.bass as bass
import concourse.mybir as mybir
from concourse.bass_utils import run_bass_kernel_spmd

F32 = mybir.dt.float32
BF16 = mybir.dt.bfloat16
AF = mybir.ActivationFunctionType
ALU = mybir.AluOpType
AX = mybir.AxisListType
EPS = 1e-6
SAME_ENGINE_SYNC = True


class Cfg:
    def __init__(s, D=2048, G=8, T=4096, NQ=4, TT=512, TH=256):
        s.D = D; s.G = G; s.T = T; s.NQ = NQ; s.TT = TT; s.TH = TH
        s.DFF = 4 * D; s.DI = 2 * D; s.NH = s.DI // 64; s.NS = 128
        assert s.NH == 8 * G
        s.KC = D // 128; s.FC = s.DFF // 128; s.IC = s.DI // 128
        s.XC = s.IC + 2 * G
        s.CK = 31; s.HW = 32; s.PRE = 64
        s.GW = G * 512
        o = 0
        def alloc(n):
            nonlocal o
            r = o; o += n; return r
        KC, XC, IC = s.KC, s.XC, s.IC
        s.pv = {}
        for nm, n in [("nmix0", KC), ("nmlp0", KC), ("bpa", KC), ("bpg", KC), ("bdw", KC), ("lng", KC),
                      ("lnb", KC), ("bpw2", KC), ("wdw", KC * 31), ("bmod0", 6 * KC),
                      ("nmix1", KC), ("nmlp1", KC), ("cw", XC * 4), ("cb", XC), ("ng", IC),
                      ("fg", KC), ("bmod1", 6 * KC)]:
            s.pv[nm] = (alloc(n), n)
        s.NV = o
        s.NR = 3 * s.NH


class Buf:
    __slots__ = ("w", "r")

    def __init__(s):
        s.w = None; s.r = {}


class TT_:
    def __init__(s, t):
        s.t = t; s.b = Buf()


class Prog:
    def __init__(s, nc, es):
        s.nc = nc; s.es = es
        s.eng = {"pe": nc.tensor, "act": nc.scalar, "dve": nc.vector, "pool": nc.gpsimd, "sp": nc.sync}
        s.sems = {}
        s.cnt = {}
        for k in s.eng:
            s.sems[k] = es.enter_context(nc.semaphore("sem_" + k)); s.cnt[k] = 0
        s.known = {k: {} for k in s.eng}
        s.nwait = 0; s.nins = 0

    def newsem(s, key):
        s.sems[key] = s.es.enter_context(s.nc.semaphore("sem_" + key)); s.cnt[key] = 0
        return key

    def _deps(s, reads, writes):
        deps = []
        for b in reads:
            if b.w: deps.append(b.w)
        for b in writes:
            if b.w: deps.append(b.w)
            deps.extend(b.r.items())
        return deps

    def _wait(s, e, deps):
        need = {}
        for k, v in deps:
            if k == e:
                if v > s.cnt[e]: continue
                if e in ("pe", "sp") or not SAME_ENGINE_SYNC: continue
            if s.known[e].get(k, 0) >= v: continue
            need[k] = max(need.get(k, 0), v)
        for k, v in need.items():
            s.eng[e].wait_ge(s.sems[k], v); s.known[e][k] = v; s.nwait += 1

    def op(s, e, fn, reads=(), writes=(), sig=True):
        s._wait(e, s._deps(reads, writes))
        ins = fn(s.eng[e]); s.nins += 1
        if sig:
            s.cnt[e] += 1; ins.then_inc(s.sems[e], 1); v = s.cnt[e]
        else:
            v = s.cnt[e] + 1
        for b in reads: b.r[e] = max(b.r.get(e, 0), v)
        for b in writes: b.w = (e, v); b.r = {}

    def dma(s, q, semkey, out, in_, reads=(), writes=()):
        s._wait(q, s._deps(reads, writes))
        ins = s.eng[q].dma_start(out=out, in_=in_); s.nins += 1
        s.cnt[semkey] += 16; ins.then_inc(s.sems[semkey], 16); v = s.cnt[semkey]
        for b in reads: b.r[semkey] = max(b.r.get(semkey, 0), v)
        for b in writes: b.w = (semkey, v); b.r = {}

    def finish(s, e, bufs):
        deps = []
        for b in bufs:
            if b.w: deps.append(b.w)
            deps.extend(b.r.items())
        s._wait(e, deps)


def build(cfg, phase):
    c = cfg
    D, KC, FC, IC, XC, G, NH, T, TT, TH, GW = c.D, c.KC, c.FC, c.IC, c.XC, c.G, c.NH, c.T, c.TT, c.TH, c.GW
    NQ = c.NQ
    nc = bass.Bass("TRN2", target_bir_lowering=False)
    dr = {}

    def din(name, shape, dt=F32):
        dr[name] = nc.dram_tensor(name, list(shape), dt, kind="ExternalInput").ap(); return dr[name]

    def dout(name, shape, dt=F32):
        dr[name] = nc.dram_tensor(name, list(shape), dt, kind="ExternalOutput").ap(); return dr[name]

    din("pvec", [128, c.NV]); din("rowp", [1, c.NR]); din("cvec", [128, KC]); din("hmask", [128, 1])
    din("cstf", [128, 4 * 128])
    if phase == "AB":
        din("x_fm", [D, T]); din("x_halo", [D, c.PRE])
        din("w_mod0", [D, 6 * D]); din("w_mod1", [D, 6 * D])
        din("pw1p", [D, 2 * D]); din("pw2", [D, D]); din("w1_0", [D, c.DFF]); din("w2_0", [c.DFF, D])
        din("w_xbc", [D, XC * 128]); din("w_dt", [D, NH])
        dout("h1", [D, T]); dout("s_loc", [128, GW]); dout("atot", [128, NH]); dout("rawhalo", [128, XC * 3])
    else:
        din("h1", [D, T]); din("rawhalo", [128, XC * 3]); din("s_all", [NQ, 128, GW]); din("atot_all", [128, NQ * NH])
        din("inc", [128, NQ * NQ]); din("valid", [128, NQ])
        din("w_mod1", [D, 6 * D]); din("w_xbc", [D, XC * 128]); din("w_z", [D, c.DI]); din("w_dt", [D, NH])
        din("w_out", [c.DI, D]); din("w1_1", [D, c.DFF]); din("w2_1", [c.DFF, D])
        dout("out_fm", [D, T])

    es = ExitStack()
    with es:
        P = Prog(nc, es)
        blk = es.enter_context(nc.Block())

        def sb(name, shape, dt=F32):
            return TT_(es.enter_context(nc.sbuf_tensor(name, list(shape), dt)))

        h = sb("h", [128, KC, TT]); hB = [Buf() for _ in range(KC)]
        u = sb("u", [128, KC, TT], BF16); uB = [Buf() for _ in range(KC)]
        BIGB = max(FC * TT * 2, 1)
        big = es.enter_context(nc.sbuf_tensor("big", [128, BIGB // 2], BF16))
        NWB = 2
        KBS = min(16, KC)
        wbuf = [sb(f"wb{i}", [128, 16, 512], BF16) for i in range(NWB)]
        wsem = [P.newsem(f"w{i}") for i in range(NWB)]
        pvec = sb("pvec_sb", [128, c.NV]); rowp = sb("rowp_sb", [128, c.NR])
        cstf = sb("cstf_sb", [128, 512]); identb = sb("identb", [128, 128], BF16)
        misc = sb("misc", [128, 8]); hmask = sb("hmask_sb", [128, 1])
        modv = [sb(f"mod{i}", [128, 6 * KC]) for i in range(2)]
        der = sb("der", [128, 6 * KC])
        scv = sb("scv", [128, KC], BF16); cvt = sb("cvt", [128, KC])
        rstd = sb("rstd", [128, TT]); tmpA = [sb(f"tmpA{i}", [128, TT + 4]) for i in range(2)]
        sqs = [sb(f"sqs{i}", [128, TT]) for i in range(2)]
        state = sb("state", [128, GW]); stB = [Buf() for _ in range(G)]
        rawhalo = sb("rawhalo_sb", [128, XC, 3]); rhB = [Buf() for _ in range(XC)]
        rawb = tmpA; accb = sqs
        wdt = sb("wdt", [128, KC, NH], BF16)
        sm = {n: sb("sm_" + n, [128, NH]) for n in ["t1", "t2", "dt", "a", "acs", "eac", "d", "ed", "wend", "etot", "atot"]}
        xs = sb("xs", [128, 640], BF16); xw = sb("xw", [128, 512], BF16)
        ld_sem = P.newsem("ld"); st_sem = P.newsem("st"); c_sem = P.newsem("cld"); cw_sem = P.newsem("cwl"); lds_sem = P.newsem("lds"); strh_sem = P.newsem("strh"); sts_sem = P.newsem("sts"); sta_sem = P.newsem("sta"); c2_sem = P.newsem("cld2")

        ident_f = cstf.t[:, 0:128]; ones_f = cstf.t[:, 128:256]; triu_f = cstf.t[:, 256:384]; ustr_f = cstf.t[:, 384:512]
        epsT = misc.t[:, 0:1]; oneT = misc.t[:, 1:2]

        def pvc(nm, i=0, n=1):
            o, _ = c.pv[nm]
            return pvec.t[:, o + i:o + i + n]

        banks = [TT_(es.enter_context(nc.psum_tensor(f"bank{i}", [128, 512], F32))) for i in range(6)]
        segp = TT_(es.enter_context(nc.psum_tensor("segp", [128, 1024], F32)))
        mm = banks[0:4]; aux0 = banks[4]; aux1 = banks[5]

        P.dma("sp", c_sem, pvec.t[:], dr["pvec"][:], writes=[pvec.b])
        P.dma("sp", c_sem, rowp.t[:], dr["rowp"][0:1, :].partition_broadcast(128), writes=[rowp.b])
        P.dma("sp", c_sem, cstf.t[:], dr["cstf"][:], writes=[cstf.b])
        P.dma("sp", c_sem, cvt.t[:], dr["cvec"][:], writes=[cvt.b])
        P.dma("sp", c_sem, hmask.t[:], dr["hmask"][:], writes=[hmask.b])
        for tt_ in (pvec, rowp, cstf, cvt, hmask):
            tt_.b.w = (c_sem, P.cnt[c_sem])
        P.op("dve", lambda e: e.memset(misc.t[:, 0:1], EPS), writes=[misc.b])
        P.op("dve", lambda e: e.memset(misc.t[:, 1:2], 1.0), writes=[misc.b])
        P.op("dve", lambda e: e.tensor_copy(out=identb.t[:], in_=ident_f), reads=[cstf.b], writes=[identb.b])
        P.op("act", lambda e: e.activation(out=scv.t[:], in_=cvt.t[:], func=AF.Silu), reads=[cvt.b], writes=[scv.b])
        P.op("act", lambda e: e.activation(out=rowp.t[:, NH:2 * NH], in_=rowp.t[:, NH:2 * NH], func=AF.Exp),
             reads=[rowp.b], writes=[rowp.b])
        P.op("dve", lambda e: e.tensor_scalar(out=rowp.t[:, NH:2 * NH], in0=rowp.t[:, NH:2 * NH], scalar1=-1.0,
                                              scalar2=None, op0=ALU.mult), reads=[rowp.b], writes=[rowp.b])
        dtb_bc = rowp.t[:, 0:NH]; aneg_bc = rowp.t[:, NH:2 * NH]; dvec_bc = rowp.t[:, 2 * NH:3 * NH]

        wcast = {}

        def precast(name):
            src = dr[name]
            dst = nc.dram_tensor(name + "_bf", list(src.shape), BF16, kind="Internal").ap()
            b = Buf(); sk = P.newsem("pc_" + name)
            R = src.shape[0]
            step = 2048
            for r0 in range(0, R, step):
                r1 = min(R, r0 + step)
                P.dma("pool", sk, dst[r0:r1, :], src[r0:r1, :], writes=[])
            b.w = (sk, P.cnt[sk])
            wcast[name] = (dst, b)

        wstate = {"i": 0}

        def wload(src, k0c, nkc, n0, ncols):
            i = wstate["i"] % NWB; wstate["i"] += 1
            wb = wbuf[i]
            rd = []
            if isinstance(src, str):
                src, cb_ = wcast[src]; rd = [cb_]
            P.dma("pool", wsem[i], wb.t[:, 0:nkc, 0:ncols],
                  src[k0c * 128:(k0c + nkc) * 128, n0:n0 + ncols].rearrange("(kc p) n -> p kc n", p=128),
                  reads=rd, writes=[wb.b])
            return wb

        def proj_A(src, K_chunks, ncols_total, rhs_fn, NT, evac, col0=0):
            nkb = (K_chunks + KBS - 1) // KBS
            for nb in range(ncols_total // 512):
                for kb in range(nkb):
                    nkc = min(KBS, K_chunks - kb * KBS)
                    wb = wload(src, kb * KBS, nkc, col0 + nb * 512, 512)
                    for j in range(4):
                        for kc in range(nkc):
                            ap, rb = rhs_fn(kb * KBS + kc)
                            first = (kb == 0 and kc == 0); last = (kb == nkb - 1 and kc == nkc - 1)
                            P.op("pe", lambda e: e.matmul(out=mm[j].t[:, :NT], lhsT=wb.t[:, kc, j * 128:(j + 1) * 128],
                                                          rhs=ap, start=first, stop=last),
                                 reads=[wb.b, rb], writes=[mm[j].b], sig=(last or (j == 3 and kc == nkc - 1)))
                for j in range(4):
                    evac(nb, j, mm[j])

        def compute_mod(layer, wsrc):
            mv = modv[layer]
            for nb in range(6 * D // 512):
                wb = wload(wsrc, 0, KC, nb * 512, 512)
                for j in range(4):
                    for kc in range(KC):
                        P.op("pe", lambda e: e.matmul(out=aux0.t[:, j:j + 1], lhsT=wb.t[:, kc, j * 128:(j + 1) * 128],
                                                      rhs=scv.t[:, kc:kc + 1], start=(kc == 0), stop=(kc == KC - 1)),
                             reads=[wb.b, scv.b], writes=[aux0.b], sig=(kc == KC - 1))
                o, _ = c.pv["bmod%d" % layer]
                P.op("dve", lambda e: e.tensor_tensor(out=mv.t[:, nb * 4:nb * 4 + 4], in0=aux0.t[:, 0:4],
                                                      in1=pvec.t[:, o + nb * 4:o + nb * 4 + 4], op=ALU.add),
                     reads=[aux0.b, pvec.b], writes=[mv.b])

        def derive(layer):
            mv = modv[layer]
            for (dst, nm, sc0) in [(0, "nmix%d" % layer, KC), (KC, "nmlp%d" % layer, 4 * KC)]:
                P.op("dve", lambda e: e.scalar_tensor_tensor(out=der.t[:, dst:dst + KC], in0=mv.t[:, sc0:sc0 + KC], scalar=1.0,
                                                             in1=pvc(nm, 0, KC), op0=ALU.add, op1=ALU.mult),
                     reads=[mv.b, pvec.b], writes=[der.b])
            if layer == 0:
                P.op("dve", lambda e: e.tensor_tensor(out=der.t[:, 2 * KC:3 * KC], in0=mv.t[:, 2 * KC:3 * KC],
                                                      in1=pvc("bpw2", 0, KC), op=ALU.mult),
                     reads=[mv.b, pvec.b], writes=[der.b])

        def rmsnorm_to(NT, src_fn, scale_fn, bias_fn, dst_fn, t0=0):
            for cc in range(KC):
                sq = sqs[cc % 2]
                ap, rb = src_fn(cc)
                P.op("act", lambda e: e.activation(out=sq.t[:, :NT], in_=ap, func=AF.Square), reads=[rb], writes=[sq.b])
                P.op("pe", lambda e: e.matmul(out=aux0.t[:, :NT], lhsT=ones_f, rhs=sq.t[:, :NT], start=(cc == 0),
                                              stop=(cc == KC - 1)), reads=[sq.b, cstf.b], writes=[aux0.b])
            P.op("act", lambda e: e.activation(out=rstd.t[:, :NT], in_=aux0.t[:, :NT], func=AF.Sqrt, scale=1.0 / D, bias=epsT),
                 reads=[aux0.b, misc.b], writes=[rstd.b])
            P.op("dve", lambda e: e.reciprocal(out=rstd.t[:, :NT], in_=rstd.t[:, :NT]), reads=[rstd.b], writes=[rstd.b])
            for cc in range(KC):
                tp = tmpA[cc % 2]
                ap, rb = src_fn(cc)
                P.op("dve", lambda e: e.tensor_tensor(out=tp.t[:, :NT], in0=ap, in1=rstd.t[:, :NT], op=ALU.mult),
                     reads=[rb, rstd.b], writes=[tp.b])
                dap, db = dst_fn(cc)
                sc_ap, sc_b = scale_fn(cc)
                if bias_fn is None:
                    P.op("act", lambda e: e.activation(out=dap, in_=tp.t[:, :NT], func=AF.Copy, scale=sc_ap),
                         reads=[tp.b, sc_b], writes=[db])
                else:
                    bi_ap, bi_b = bias_fn(cc)
                    P.op("act", lambda e: e.activation(out=dap, in_=tp.t[:, :NT], func=AF.Identity, scale=sc_ap, bias=bi_ap),
                         reads=[tp.b, sc_b, bi_b], writes=[db])

        def h_src(NT, t0=0):
            return lambda cc: (h.t[:, cc, t0:t0 + NT], hB[cc])

        def u_dst(NT, t0=0):
            return lambda cc: (u.t[:, cc, t0:t0 + NT], uB[cc])

        def u_rhs(NT, t0=0):
            return lambda kc: (u.t[:, kc, t0:t0 + NT], uB[kc])

        def mlp(NT, layer, w1src, w2src):
            mv = modv[layer]
            rmsnorm_to(NT, h_src(NT), lambda cc: (der.t[:, KC + cc:KC + cc + 1], der.b),
                       lambda cc: (mv.t[:, 3 * KC + cc:3 * KC + cc + 1], mv.b), u_dst(NT))
            hid = big[:, 0:FC * TT].rearrange("p (c t) -> p c t", t=TT)

            def ev1(nb, j, bank):
                fc = nb * 4 + j
                sq = sqs[fc % 2]
                P.op("act", lambda e: e.activation(out=sq.t[:, :NT], in_=bank.t[:, :NT], func=AF.Square),
                     reads=[bank.b], writes=[sq.b])
                P.op("dve", lambda e: e.scalar_tensor_tensor(out=hid[:, fc, :NT], in0=bank.t[:, :NT], scalar=0.0,
                                                             in1=sq.t[:, :NT], op0=ALU.is_gt, op1=ALU.mult),
                     reads=[bank.b, sq.b], writes=[bigB[fc]])
            proj_A(w1src, KC, c.DFF, u_rhs(NT), NT, ev1)

            def ev2(nb, j, bank):
                dc = nb * 4 + j
                P.op("dve", lambda e: e.scalar_tensor_tensor(out=h.t[:, dc, :NT], in0=bank.t[:, :NT],
                                                             scalar=mv.t[:, 5 * KC + dc:5 * KC + dc + 1],
                                                             in1=h.t[:, dc, :NT], op0=ALU.mult, op1=ALU.add),
                     reads=[bank.b, mv.b, hB[dc]], writes=[hB[dc]])
            proj_A(w2src, FC, D, lambda kc: (hid[:, kc, :NT], bigB[kc]), NT, ev2)

        bigB = [Buf() for _ in range(max(FC, 64))]

        XS = TT if phase == "AB" else TH
        xbcT = big[:, 0:XC * XS].rearrange("p (c t) -> p c t", t=XS)
        ZOFF = XC * XS
        NCH_MAX = TT // 128

        def mamba_inproj_xbc(NT, t0, nchunks_xbc):
            def ev(nb, j, bank):
                cc = nb * 4 + j
                rw = rawb[cc % 2]; ac = accb[cc % 2]
                P.op("act", lambda e: e.activation(out=rw.t[:, 0:3], in_=rawhalo.t[:, cc, :], func=AF.Copy),
                     reads=[rhB[cc]], writes=[rw.b])
                P.op("act", lambda e: e.activation(out=rw.t[:, 3:3 + NT], in_=bank.t[:, :NT], func=AF.Copy),
                     reads=[bank.b], writes=[rw.b])
                P.op("act", lambda e: e.activation(out=rawhalo.t[:, cc, :], in_=rw.t[:, NT:NT + 3], func=AF.Copy),
                     reads=[rw.b], writes=[rhB[cc]])
                P.op("dve", lambda e: e.tensor_scalar(out=ac.t[:, :NT], in0=rw.t[:, 0:NT], scalar1=pvc("cw", cc * 4 + 0),
                                                      scalar2=pvc("cb", cc), op0=ALU.mult, op1=ALU.add),
                     reads=[rw.b, pvec.b], writes=[ac.b])
                for k in range(1, 4):
                    P.op("dve", lambda e: e.scalar_tensor_tensor(out=ac.t[:, :NT], in0=rw.t[:, k:k + NT],
                                                                 scalar=pvc("cw", cc * 4 + k), in1=ac.t[:, :NT],
                                                                 op0=ALU.mult, op1=ALU.add),
                         reads=[rw.b, pvec.b, ac.b], writes=[ac.b])
                P.op("act", lambda e: e.activation(out=xbcT[:, cc, :NT], in_=ac.t[:, :NT], func=AF.Silu),
                     reads=[ac.b], writes=[bigB[cc]])
            proj_A("w_xbc", KC, ((nchunks_xbc + 3) // 4) * 512, u_rhs(NT, t0), NT, ev)

        def ssd_dt(t0, tc):
            for kc in range(KC):
                P.op("pe", lambda e: e.matmul(out=aux0.t[:, 0:NH], lhsT=u.t[:, kc, t0 + tc * 128:t0 + (tc + 1) * 128],
                                              rhs=wdt.t[:, kc, :], start=(kc == 0), stop=(kc == KC - 1)),
                     reads=[uB[kc], wdt.b], writes=[aux0.b], sig=(kc == KC - 1))
            S = sm
            P.op("dve", lambda e: e.tensor_tensor(out=S["t1"].t[:], in0=aux0.t[:, 0:NH], in1=dtb_bc, op=ALU.add),
                 reads=[aux0.b, rowp.b], writes=[S["t1"].b])
            P.op("act", lambda e: e.activation(out=S["t2"].t[:], in_=S["t1"].t[:], func=AF.Exp), reads=[S["t1"].b], writes=[S["t2"].b])
            P.op("act", lambda e: e.activation(out=S["dt"].t[:], in_=S["t2"].t[:], func=AF.Ln, bias=oneT),
                 reads=[S["t2"].b, misc.b], writes=[S["dt"].b])
            P.op("dve", lambda e: e.tensor_tensor(out=S["a"].t[:], in0=S["dt"].t[:], in1=aneg_bc, op=ALU.mult),
                 reads=[S["dt"].b, rowp.b], writes=[S["a"].b])
            P.op("pe", lambda e: e.matmul(out=aux0.t[:, 64:64 + NH], lhsT=triu_f, rhs=S["a"].t[:], start=True, stop=True),
                 reads=[S["a"].b, cstf.b], writes=[aux0.b])
            P.op("pe", lambda e: e.matmul(out=aux0.t[:, 128:128 + NH], lhsT=ones_f, rhs=S["a"].t[:], start=True, stop=True),
                 reads=[S["a"].b, cstf.b], writes=[aux0.b])
            P.op("act", lambda e: e.activation(out=S["acs"].t[:], in_=aux0.t[:, 64:64 + NH], func=AF.Copy),
                 reads=[aux0.b], writes=[S["acs"].b])
            P.op("act", lambda e: e.activation(out=S["eac"].t[:], in_=aux0.t[:, 64:64 + NH], func=AF.Exp),
                 reads=[aux0.b], writes=[S["eac"].b])
            P.op("dve", lambda e: e.tensor_tensor(out=S["d"].t[:], in0=aux0.t[:, 128:128 + NH], in1=S["acs"].t[:], op=ALU.subtract),
                 reads=[aux0.b, S["acs"].b], writes=[S["d"].b])
            P.op("act", lambda e: e.activation(out=S["ed"].t[:], in_=S["d"].t[:], func=AF.Exp), reads=[S["d"].b], writes=[S["ed"].b])
            P.op("dve", lambda e: e.tensor_tensor(out=S["wend"].t[:], in0=S["ed"].t[:], in1=S["dt"].t[:], op=ALU.mult),
                 reads=[S["ed"].b, S["dt"].b], writes=[S["wend"].b])
            P.op("act", lambda e: e.activation(out=S["etot"].t[:], in_=aux0.t[:, 128:128 + NH], func=AF.Exp),
                 reads=[aux0.b], writes=[S["etot"].b])
            P.op("dve", lambda e: e.tensor_tensor(out=S["atot"].t[:], in0=S["atot"].t[:], in1=aux0.t[:, 128:128 + NH], op=ALU.add),
                 reads=[aux0.b, S["atot"].b], writes=[S["atot"].b])

        aux1b = aux1.t[:].bitcast(BF16)

        def ssd_tokmajor(g, tc):
            for j in range(4):
                P.op("pe", lambda e: e.transpose(out=aux1b[:, j * 128:(j + 1) * 128],
                                                 in_=xbcT[:, g * 4 + j, tc * 128:(tc + 1) * 128], identity=identb.t[:]),
                     reads=[bigB[g * 4 + j], identb.b], writes=[aux1.b], sig=False)
            P.op("pe", lambda e: e.transpose(out=aux1b[:, 512:640], in_=xbcT[:, IC + g, tc * 128:(tc + 1) * 128],
                                             identity=identb.t[:]), reads=[bigB[IC + g], identb.b], writes=[aux1.b])
            P.op("act", lambda e: e.activation(out=xs.t[:, 0:640], in_=aux1b[:, 0:640], func=AF.Copy),
                 reads=[aux1.b], writes=[xs.b])

        def bc8(ap, n):
            return ap.unsqueeze(2).to_broadcast([128, 8, n])

        def ssd_state_update(g):
            S = sm
            P.op("dve", lambda e: e.tensor_tensor(out=xw.t[:].rearrange("p (h q) -> p h q", q=64),
                                                  in0=xs.t[:, 0:512].rearrange("p (h q) -> p h q", q=64),
                                                  in1=bc8(S["wend"].t[:, g * 8:(g + 1) * 8], 64), op=ALU.mult),
                 reads=[xs.b, S["wend"].b], writes=[xw.b])
            P.op("pe", lambda e: e.matmul(out=mm[2].t[:, :], lhsT=xs.t[:, 512:640], rhs=xw.t[:], start=True, stop=True),
                 reads=[xs.b, xw.b], writes=[mm[2].b])
            stg = state.t[:, g * 512:(g + 1) * 512]
            P.op("dve", lambda e: e.tensor_tensor(out=stg.rearrange("p (h q) -> p h q", q=64),
                                                  in0=stg.rearrange("p (h q) -> p h q", q=64),
                                                  in1=bc8(S["etot"].t[:, g * 8:(g + 1) * 8], 64), op=ALU.mult),
                 reads=[stB[g], S["etot"].b], writes=[stB[g]])
            P.op("dve", lambda e: e.tensor_tensor(out=stg, in0=stg, in1=mm[2].t[:, :], op=ALU.add),
                 reads=[stB[g], mm[2].b], writes=[stB[g]])

        if phase == "AB":
            vpad = big[:, 0:KC * (c.HW + TT)].rearrange("p (c t) -> p c t", t=c.HW + TT)
            VB = [Buf() for _ in range(KC)]
            CVO_OFF = KC * (c.HW + TT)
            CVO_OFF += CVO_OFF % 2
            cvo = big[:, CVO_OFF:CVO_OFF + 2 * KC * TT].bitcast(F32).rearrange("p (c t) -> p c t", t=TT)
            CB = [Buf() for _ in range(KC)]
            dgs = [sb(f"dg{i}", [128, 16, 128], BF16) for i in range(2)]
            mean_t = sb("mean_t", [128, TT]); msq = sb("msq", [128, TT])
            P.dma("pool", cw_sem, wdt.t[:], dr["w_dt"].rearrange("(kc p) n -> p kc n", p=128), writes=[wdt.b])
            for nm_ in ["pw1p", "pw2", "w1_0", "w2_0", "w_xbc"]:
                precast(nm_)
            compute_mod(0, dr["w_mod0"]); compute_mod(1, dr["w_mod1"])
            P.op("dve", lambda e: e.memset(state.t[:], 0.0), writes=stB)
            P.op("dve", lambda e: e.memset(sm["atot"].t[:], 0.0), writes=[sm["atot"].b])
            P.op("dve", lambda e: e.memset(rawhalo.t[:], 0.0), writes=rhB)
            P.op("dve", lambda e: e.memset(vpad[:, :, 0:c.HW], 0.0), writes=VB)

            def layer0(NT):
                mv = modv[0]
                rmsnorm_to(NT, h_src(NT), lambda cc: (der.t[:, cc:cc + 1], der.b),
                           lambda cc: (mv.t[:, cc:cc + 1], mv.b), u_dst(NT))

                def evg(nb, j, bank):
                    if j < 2: return
                    jj = j - 2; cc = nb * 2 + jj
                    sg = tmpA[cc % 2]
                    P.op("act", lambda e: e.activation(out=sg.t[:, :NT], in_=bank.t[:, :NT], func=AF.Sigmoid, bias=pvc("bpg", cc)),
                         reads=[bank.b, pvec.b], writes=[sg.b])
                    P.op("dve", lambda e: e.scalar_tensor_tensor(out=vpad[:, cc, c.HW:c.HW + NT], in0=mm[jj].t[:, :NT],
                                                                 scalar=pvc("bpa", cc), in1=sg.t[:, :NT],
                                                                 op0=ALU.add, op1=ALU.mult),
                         reads=[mm[jj].b, pvec.b, sg.b], writes=[VB[cc]])
                proj_A("pw1p", KC, 2 * D, u_rhs(NT), NT, evg)

                for cc in range(KC):
                    o, _ = c.pv["wdw"]
                    bank = mm[cc % 4]
                    for hf, (k0, k1) in enumerate([(0, 16), (16, 31)]):
                        dg = dgs[hf]; nk = k1 - k0
                        P.op("dve", lambda e: e.tensor_tensor(out=dg.t[:, 0:nk, :], in0=ident_f.unsqueeze(1).to_broadcast([128, nk, 128]),
                                                              in1=pvec.t[:, o + cc * 31 + k0:o + cc * 31 + k1].unsqueeze(2).to_broadcast([128, nk, 128]),
                                                              op=ALU.mult), reads=[cstf.b, pvec.b], writes=[dg.b])
                        for k in range(k0, k1):
                            P.op("pe", lambda e: e.matmul(out=bank.t[:, :NT], lhsT=dg.t[:, k - k0, :],
                                                          rhs=vpad[:, cc, c.HW - 30 + k:c.HW - 30 + k + NT], start=(k == 0), stop=(k == 30)),
                                 reads=[dg.b, VB[cc]], writes=[bank.b], sig=(k == k1 - 1))
                    sq = sqs[cc % 2]
                    P.op("act", lambda e: e.activation(out=cvo[:, cc, :NT], in_=bank.t[:, :NT], func=AF.Identity, bias=pvc("bdw", cc)),
                         reads=[bank.b, pvec.b], writes=[CB[cc]])
                    P.op("act", lambda e: e.activation(out=sq.t[:, :NT], in_=bank.t[:, :NT], func=AF.Square, bias=pvc("bdw", cc)),
                         reads=[bank.b, pvec.b], writes=[sq.b])
                    P.op("pe", lambda e: e.matmul(out=aux0.t[:, :NT], lhsT=ones_f, rhs=cvo[:, cc, :NT], start=(cc == 0), stop=(cc == KC - 1)),
                         reads=[CB[cc], cstf.b], writes=[aux0.b])
                    P.op("pe", lambda e: e.matmul(out=aux1.t[:, :NT], lhsT=ones_f, rhs=sq.t[:, :NT], start=(cc == 0), stop=(cc == KC - 1)),
                         reads=[sq.b, cstf.b], writes=[aux1.b])
                P.op("dve", lambda e: e.tensor_scalar(out=mean_t.t[:, :NT], in0=aux0.t[:, :NT], scalar1=1.0 / D, scalar2=None, op0=ALU.mult),
                     reads=[aux0.b], writes=[mean_t.b])
                P.op("dve", lambda e: e.tensor_tensor(out=msq.t[:, :NT], in0=mean_t.t[:, :NT], in1=mean_t.t[:, :NT], op=ALU.mult),
                     reads=[mean_t.b], writes=[msq.b])
                P.op("dve", lambda e: e.scalar_tensor_tensor(out=msq.t[:, :NT], in0=aux1.t[:, :NT], scalar=1.0 / D, in1=msq.t[:, :NT],
                                                             op0=ALU.mult, op1=ALU.subtract), reads=[aux1.b, msq.b], writes=[msq.b])
                P.op("act", lambda e: e.activation(out=rstd.t[:, :NT], in_=msq.t[:, :NT], func=AF.Sqrt, bias=epsT),
                     reads=[msq.b, misc.b], writes=[rstd.b])
                P.op("dve", lambda e: e.reciprocal(out=rstd.t[:, :NT], in_=rstd.t[:, :NT]), reads=[rstd.b], writes=[rstd.b])
                for cc in range(KC):
                    tp = tmpA[cc % 2]
                    P.op("dve", lambda e: e.tensor_tensor(out=tp.t[:, :NT], in0=cvo[:, cc, :NT], in1=mean_t.t[:, :NT], op=ALU.subtract),
                         reads=[CB[cc], mean_t.b], writes=[tp.b])
                    P.op("dve", lambda e: e.tensor_tensor(out=tp.t[:, :NT], in0=tp.t[:, :NT], in1=rstd.t[:, :NT], op=ALU.mult),
                         reads=[tp.b, rstd.b], writes=[tp.b])
                    P.op("act", lambda e: e.activation(out=u.t[:, cc, :NT], in_=tp.t[:, :NT], func=AF.Silu, scale=pvc("lng", cc), bias=pvc("lnb", cc)),
                         reads=[tp.b, pvec.b], writes=[uB[cc]])

                def ev2(nb, j, bank):
                    dc = nb * 4 + j
                    tp = tmpA[dc % 2]
                    P.op("act", lambda e: e.activation(out=tp.t[:, :NT], in_=bank.t[:, :NT], func=AF.Identity,
                                                       scale=mv.t[:, 2 * KC + dc:2 * KC + dc + 1], bias=der.t[:, 2 * KC + dc:2 * KC + dc + 1]),
                         reads=[bank.b, mv.b, der.b], writes=[tp.b])
                    P.op("dve", lambda e: e.tensor_tensor(out=h.t[:, dc, :NT], in0=h.t[:, dc, :NT], in1=tp.t[:, :NT], op=ALU.add),
                         reads=[tp.b, hB[dc]], writes=[hB[dc]])
                proj_A("pw2", KC, D, u_rhs(NT), NT, ev2)

            def save_vhalo(NT, masked):
                for cc in range(KC):
                    if masked:
                        P.op("dve", lambda e: e.tensor_scalar(out=vpad[:, cc, 0:c.HW], in0=vpad[:, cc, NT:NT + c.HW], scalar1=hmask.t[:, 0:1],
                                                              scalar2=None, op0=ALU.mult), reads=[VB[cc], hmask.b], writes=[VB[cc]])
                    else:
                        P.op("dve", lambda e: e.tensor_copy(out=vpad[:, cc, 0:c.HW], in_=vpad[:, cc, NT:NT + c.HW]),
                             reads=[VB[cc]], writes=[VB[cc]])

            vh = sb("vh", [128, KC, c.HW], BF16)

            tiles = [("pre", 0, c.PRE)] + [("main", i * TT, TT) for i in range(T // TT)]
            for (kind, t0, NT) in tiles:
                src = dr["x_halo"] if kind == "pre" else dr["x_fm"]
                P.dma("sp", ld_sem, h.t[:, :, 0:NT], src[:, t0:t0 + NT].rearrange("(c p) t -> p c t", p=128), writes=hB)
                derive(0)
                if kind == "pre":
                    P.op("dve", lambda e: e.memset(vpad[:, :, 0:c.HW], 0.0), writes=VB)
                else:
                    P.op("dve", lambda e: e.tensor_copy(out=vpad[:, :, 0:c.HW], in_=vh.t[:]), reads=[vh.b], writes=VB)
                layer0(NT)
                if kind == "pre":
                    P.op("dve", lambda e: e.tensor_scalar(out=vh.t[:], in0=vpad[:, :, NT:NT + c.HW], scalar1=hmask.t[:, 0:1], scalar2=None,
                                                          op0=ALU.mult), reads=VB + [hmask.b], writes=[vh.b])
                else:
                    P.op("dve", lambda e: e.tensor_copy(out=vh.t[:], in_=vpad[:, :, NT:NT + c.HW]), reads=VB, writes=[vh.b])
                mlp(NT, 0, "w1_0", "w2_0")
                if kind == "main":
                    P.dma("sp", st_sem, dr["h1"][:, t0:t0 + NT].rearrange("(c p) t -> p c t", p=128), h.t[:, :, 0:NT], reads=hB)
                derive(1)
                mv = modv[1]
                rmsnorm_to(NT, h_src(NT), lambda cc: (der.t[:, cc:cc + 1], der.b), lambda cc: (mv.t[:, cc:cc + 1], mv.b), u_dst(NT))
                if kind == "pre":
                    mamba_inproj_xbc(NT, 0, XC)
                    for cc in range(XC):
                        P.op("dve", lambda e: e.tensor_scalar(out=rawhalo.t[:, cc, :], in0=rawhalo.t[:, cc, :], scalar1=hmask.t[:, 0:1],
                                                              scalar2=None, op0=ALU.mult), reads=[rhB[cc], hmask.b], writes=[rhB[cc]])
                    P.dma("sp", strh_sem, dr["rawhalo"][:], rawhalo.t[:].rearrange("p c k -> p (c k)"), reads=rhB)
                else:
                    mamba_inproj_xbc(NT, 0, IC + G)
                    for tc in range(NT // 128):
                        ssd_dt(0, tc)
                        for g in range(G):
                            ssd_tokmajor(g, tc)
                            ssd_state_update(g)
            P.dma("sp", sts_sem, dr["s_loc"][:], state.t[:], reads=stB)
            P.dma("sp", sta_sem, dr["atot"][:], sm["atot"].t[:], reads=[sm["atot"].b])
            P.finish("sp", hB + stB + [sm["atot"].b] + rhB)
            P.eng["sp"].wait_ge(P.sems[st_sem], P.cnt[st_sem])

        else:
            NCH = TH // 128
            zs = big[:, ZOFF:ZOFF + NCH * c.DI].rearrange("p (t c) -> p t c", c=c.DI)
            ZB = [Buf() for _ in range(NCH)]
            YOFF = ZOFF + NCH * c.DI
            assert YOFF + IC * TH <= BIGB // 2
            ynT = TT_(big[:, YOFF:YOFF + IC * TH].rearrange("p (c t) -> p c t", t=TH)); YB = [Buf() for _ in range(IC)]
            stbf = sb("stbf", [128, 512], BF16)
            Lb = sb("Lb", [128, 8, 128]); dec = sb("dec", [128, 8, 128]); sc1 = dec
            scT = sb("scT", [128, 8, 128], BF16); cbm = sb("cbm", [128, 128])
            yb = sb("yb", [128, 512]); yt = sb("yt", [128, 512]); ssq = sb("ssq", [128, 2])
            coef = sb("coef", [128, NH]); cin = sb("cin", [128, NQ * NQ + NQ]); atl = sb("atl", [128, NQ * NH])
            stmp = TT_(big[:, 0:2 * GW].bitcast(F32))
            P.dma("pool", cw_sem, wdt.t[:], dr["w_dt"].rearrange("(kc p) n -> p kc n", p=128), writes=[wdt.b])
            P.dma("sp", c2_sem, cin.t[:, 0:NQ * NQ], dr["inc"][:], writes=[cin.b])
            P.dma("sp", c2_sem, cin.t[:, NQ * NQ:], dr["valid"][:], writes=[cin.b])
            P.dma("sp", c2_sem, atl.t[:], dr["atot_all"][:], writes=[atl.b])
            P.dma("sp", c2_sem, rawhalo.t[:].rearrange("p c k -> p (c k)"), dr["rawhalo"][:], writes=rhB)
            for b_ in [cin.b, atl.b] + rhB:
                b_.w = (c2_sem, P.cnt[c2_sem])
            for nm_ in ["w_xbc", "w_z", "w_out", "w1_1", "w2_1"]:
                precast(nm_)
            compute_mod(1, dr["w_mod1"])
            derive(1)
            P.op("dve", lambda e: e.memset(sm["atot"].t[:], 0.0), writes=[sm["atot"].b])
            P.op("dve", lambda e: e.memset(state.t[:], 0.0), writes=stB)
            for j in range(NQ):
                P.op("dve", lambda e: e.tensor_scalar(out=coef.t[:], in0=atl.t[:, 0:NH], scalar1=cin.t[:, j * NQ:j * NQ + 1], scalar2=None,
                                                      op0=ALU.mult), reads=[atl.b, cin.b], writes=[coef.b])
                for m in range(1, NQ):
                    P.op("dve", lambda e: e.scalar_tensor_tensor(out=coef.t[:], in0=atl.t[:, m * NH:(m + 1) * NH],
                                                                 scalar=cin.t[:, j * NQ + m:j * NQ + m + 1], in1=coef.t[:],
                                                                 op0=ALU.mult, op1=ALU.add), reads=[atl.b, cin.b, coef.b], writes=[coef.b])
                P.op("act", lambda e: e.activation(out=coef.t[:], in_=coef.t[:], func=AF.Exp), reads=[coef.b], writes=[coef.b])
                P.op("dve", lambda e: e.tensor_scalar(out=coef.t[:], in0=coef.t[:], scalar1=cin.t[:, NQ * NQ + j:NQ * NQ + j + 1], scalar2=None,
                                                      op0=ALU.mult), reads=[coef.b, cin.b], writes=[coef.b])
                P.dma("sp", lds_sem, stmp.t[:], dr["s_all"][j], writes=[stmp.b])
                P.op("dve", lambda e: e.tensor_tensor(out=stmp.t[:].rearrange("p (h q) -> p h q", q=64),
                                                      in0=stmp.t[:].rearrange("p (h q) -> p h q", q=64),
                                                      in1=coef.t[:].unsqueeze(2).to_broadcast([128, NH, 64]), op=ALU.mult),
                     reads=[stmp.b, coef.b], writes=[stmp.b])
                P.op("dve", lambda e: e.tensor_tensor(out=state.t[:], in0=state.t[:], in1=stmp.t[:], op=ALU.add),
                     reads=[stmp.b] + stB, writes=stB)

            def mamba_full(t0, NT):
                mamba_inproj_xbc(NT, t0, XC)
                for nb in range(c.DI // 512):
                    wb = wload("w_z", 0, KC, nb * 512, 512)
                    for tc in range(NT // 128):
                        for kc in range(KC):
                            P.op("pe", lambda e: e.matmul(out=mm[tc].t[:, :], lhsT=u.t[:, kc, t0 + tc * 128:t0 + (tc + 1) * 128],
                                                          rhs=wb.t[:, kc, :], start=(kc == 0), stop=(kc == KC - 1)),
                                 reads=[uB[kc], wb.b], writes=[mm[tc].b], sig=(kc == KC - 1))
                        P.op("act", lambda e: e.activation(out=zs[:, tc, nb * 512:(nb + 1) * 512], in_=mm[tc].t[:, :], func=AF.Silu),
                             reads=[mm[tc].b], writes=[ZB[tc]])
                S = sm
                for tc in range(NT // 128):
                    ssd_dt(t0, tc)
                    sl = slice(tc * 128, (tc + 1) * 128)
                    for g in range(G):
                        g8 = slice(g * 8, (g + 1) * 8)
                        ssd_tokmajor(g, tc)
                        P.op("act", lambda e: e.activation(out=stbf.t[:], in_=state.t[:, g * 512:(g + 1) * 512], func=AF.Copy),
                             reads=[stB[g]], writes=[stbf.b])
                        P.op("pe", lambda e: e.matmul(out=mm[1].t[:, :], lhsT=xbcT[:, IC + G + g, sl], rhs=stbf.t[:], start=True, stop=True),
                             reads=[bigB[IC + G + g], stbf.b], writes=[mm[1].b])
                        P.op("pe", lambda e: e.matmul(out=aux1.t[:, 384:512], lhsT=xbcT[:, IC + g, sl], rhs=xbcT[:, IC + G + g, sl],
                                                      start=True, stop=True), reads=[bigB[IC + g], bigB[IC + G + g]], writes=[aux1.b])
                        P.op("dve", lambda e: e.tensor_tensor(out=cbm.t[:], in0=aux1.t[:, 384:512], in1=triu_f, op=ALU.mult),
                             reads=[aux1.b, cstf.b], writes=[cbm.b])
                        P.op("dve", lambda e: e.tensor_tensor(out=Lb.t[:], in0=ustr_f.unsqueeze(1).to_broadcast([128, 8, 128]),
                                                              in1=bc8(S["a"].t[:, g8], 128), op=ALU.mult),
                             reads=[cstf.b, S["a"].b], writes=[Lb.b])
                        for hh in range(8):
                            P.op("pe", lambda e: e.matmul(out=segp.t[:, hh * 128:(hh + 1) * 128], lhsT=Lb.t[:, hh, :], rhs=triu_f,
                                                          start=True, stop=True), reads=[Lb.b, cstf.b], writes=[segp.b], sig=(hh == 7))
                        P.op("act", lambda e: e.activation(out=dec.t[:, 0:4, :], in_=segp.t[:, 0:512].rearrange("p (h l) -> p h l", l=128), func=AF.Exp),
                             reads=[segp.b], writes=[dec.b])
                        P.op("act", lambda e: e.activation(out=dec.t[:, 4:8, :], in_=segp.t[:, 512:1024].rearrange("p (h l) -> p h l", l=128), func=AF.Exp),
                             reads=[segp.b], writes=[dec.b])
                        P.op("dve", lambda e: e.tensor_tensor(out=sc1.t[:], in0=dec.t[:], in1=bc8(S["dt"].t[:, g8], 128), op=ALU.mult),
                             reads=[dec.b, S["dt"].b], writes=[sc1.b])
                        P.op("dve", lambda e: e.tensor_tensor(out=scT.t[:], in0=sc1.t[:], in1=cbm.t[:].unsqueeze(1).to_broadcast([128, 8, 128]), op=ALU.mult),
                             reads=[sc1.b, cbm.b], writes=[scT.b])
                        for hh in range(8):
                            P.op("pe", lambda e: e.matmul(out=mm[0].t[:, hh * 64:(hh + 1) * 64], lhsT=scT.t[:, hh, :], rhs=xs.t[:, hh * 64:(hh + 1) * 64],
                                                          start=True, stop=True), reads=[scT.b, xs.b], writes=[mm[0].b], sig=(hh == 7))
                        P.op("dve", lambda e: e.tensor_tensor(out=yb.t[:].rearrange("p (h q) -> p h q", q=64),
                                                              in0=mm[1].t[:, :].rearrange("p (h q) -> p h q", q=64),
                                                              in1=bc8(S["eac"].t[:, g8], 64), op=ALU.mult),
                             reads=[mm[1].b, S["eac"].b], writes=[yb.b])
                        P.op("dve", lambda e: e.tensor_tensor(out=yb.t[:], in0=yb.t[:], in1=mm[0].t[:, :], op=ALU.add),
                             reads=[yb.b, mm[0].b], writes=[yb.b])
                        P.op("dve", lambda e: e.tensor_tensor(out=yt.t[:].rearrange("p (h q) -> p h q", q=64), in0=xs.t[:, 0:512].rearrange("p (h q) -> p h q", q=64), in1=bc8(dvec_bc[:, g8], 64), op=ALU.mult),
                             reads=[xs.b, rowp.b], writes=[yt.b])
                        P.op("dve", lambda e: e.tensor_tensor(out=yb.t[:], in0=yb.t[:], in1=yt.t[:], op=ALU.add),
                             reads=[yb.b, yt.b], writes=[yb.b])
                        P.op("dve", lambda e: e.tensor_tensor(out=yb.t[:], in0=yb.t[:], in1=zs[:, tc, g * 512:(g + 1) * 512], op=ALU.mult),
                             reads=[yb.b, ZB[tc]], writes=[yb.b])
                        P.op("dve", lambda e: e.tensor_tensor(out=yt.t[:], in0=yb.t[:], in1=yb.t[:], op=ALU.mult), reads=[yb.b], writes=[yt.b])
                        P.op("dve", lambda e: e.tensor_reduce(out=ssq.t[:, 0:1], in_=yt.t[:], axis=AX.X, op=ALU.add), reads=[yt.b], writes=[ssq.b])
                        P.op("act", lambda e: e.activation(out=ssq.t[:, 1:2], in_=ssq.t[:, 0:1], func=AF.Sqrt, scale=1.0 / 512, bias=epsT),
                             reads=[ssq.b, misc.b], writes=[ssq.b])
                        P.op("dve", lambda e: e.reciprocal(out=ssq.t[:, 1:2], in_=ssq.t[:, 1:2]), reads=[ssq.b], writes=[ssq.b])
                        P.op("act", lambda e: e.activation(out=yb.t[:], in_=yb.t[:], func=AF.Copy, scale=ssq.t[:, 1:2]),
                             reads=[yb.b, ssq.b], writes=[yb.b])
                        for j in range(4):
                            P.op("pe", lambda e: e.transpose(out=mm[3].t[:, j * 128:(j + 1) * 128], in_=yb.t[:, j * 128:(j + 1) * 128], identity=ident_f),
                                 reads=[yb.b, cstf.b], writes=[mm[3].b], sig=(j == 3))
                        for j in range(4):
                            P.op("act", lambda e: e.activation(out=ynT.t[:, g * 4 + j, sl], in_=mm[3].t[:, j * 128:(j + 1) * 128], func=AF.Copy,
                                                               scale=pvc("ng", g * 4 + j)), reads=[mm[3].b, pvec.b], writes=[YB[g * 4 + j]])
                        ssd_state_update(g)
                mv = modv[1]

                def evo(nb, j, bank):
                    dc = nb * 4 + j
                    P.op("dve", lambda e: e.scalar_tensor_tensor(out=h.t[:, dc, t0:t0 + NT], in0=bank.t[:, :NT],
                                                                 scalar=mv.t[:, 2 * KC + dc:2 * KC + dc + 1], in1=h.t[:, dc, t0:t0 + NT],
                                                                 op0=ALU.mult, op1=ALU.add), reads=[bank.b, mv.b, hB[dc]], writes=[hB[dc]])
                proj_A("w_out", IC, D, lambda kc: (ynT.t[:, kc, :NT], YB[kc]), NT, evo)

            mv = modv[1]
            obuf = tmpA
            for i in range(T // TT):
                t0 = i * TT
                P.dma("sp", ld_sem, h.t[:], dr["h1"][:, t0:t0 + TT].rearrange("(c p) t -> p c t", p=128), writes=hB)
                rmsnorm_to(TT, h_src(TT), lambda cc: (der.t[:, cc:cc + 1], der.b), lambda cc: (mv.t[:, cc:cc + 1], mv.b), u_dst(TT))
                for hf in range(TT // TH):
                    mamba_full(hf * TH, TH)
                mlp(TT, 1, "w1_1", "w2_1")
                ost = big[:, 0:2 * KC * TT].bitcast(F32).rearrange("p (c t) -> p c t", t=TT)
                rmsnorm_to(TT, h_src(TT), lambda cc: (pvc("fg", cc), pvec.b), None, lambda cc: (ost[:, cc, :], bigB[cc]))
                P.dma("sp", st_sem, dr["out_fm"][:, t0:t0 + TT].rearrange("(c p) t -> p c t", p=128), ost, reads=bigB)
            P.finish("sp", bigB)
            P.eng["sp"].wait_ge(P.sems[st_sem], P.cnt[st_sem])
    nc._prog_stats = (P.nins, P.nwait)
    return nc


def _fm(v, n=None):
    v = np.asarray(v, np.float32)
    return np.ascontiguousarray(v.reshape(-1, 128).T)


def _consts():
    i = np.arange(128)
    ident = np.eye(128, dtype=np.float32)
    ones = np.ones((128, 128), np.float32)
    triu = (i[:, None] <= i[None, :]).astype(np.float32)
    ustr = (i[:, None] > i[None, :]).astype(np.float32)
    return np.ascontiguousarray(np.concatenate([ident, ones, triu, ustr], axis=1))


def host_prep(cfg, inp, n_cores):
    c = cfg
    D, KC, XC, IC, G, NH, T = c.D, c.KC, c.XC, c.IC, c.G, c.NH, c.T
    f = lambda a: np.asarray(a, np.float32)
    pv = np.zeros((128, c.NV), np.float32)

    def put(nm, arr):
        o, n = c.pv[nm]
        assert arr.shape == (128, n), (nm, arr.shape, n)
        pv[:, o:o + n] = arr
    put("nmix0", _fm(inp["norm_mix_g"][0])); put("nmlp0", _fm(inp["norm_mlp_g"][0]))
    b1 = f(inp["cf_b_pw1"][0]); put("bpa", _fm(b1[:D])); put("bpg", _fm(b1[D:]))
    put("bdw", _fm(inp["cf_b_dw"][0])); put("lng", _fm(inp["cf_ln_g"][0])); put("lnb", _fm(inp["cf_ln_b"][0]))
    put("bpw2", _fm(inp["cf_b_pw2"][0]))
    wdw = f(inp["cf_w_dw"][0])
    put("wdw", np.ascontiguousarray(wdw.T.reshape(KC, 128, 31).transpose(1, 0, 2).reshape(128, KC * 31)))
    put("bmod0", _fm(inp["b_mod"][0])); put("bmod1", _fm(inp["b_mod"][1]))
    put("nmix1", _fm(inp["norm_mix_g"][1])); put("nmlp1", _fm(inp["norm_mlp_g"][1]))
    cw = f(inp["mb_conv_w"][0])
    put("cw", np.ascontiguousarray(cw.T.reshape(XC, 128, 4).transpose(1, 0, 2).reshape(128, XC * 4)))
    put("cb", _fm(inp["mb_conv_b"][0])); put("ng", _fm(inp["mb_norm_g"][0])); put("fg", _fm(inp["final_norm_g"]))
    rowp = np.concatenate([f(inp["mb_dt_bias"][0]), f(inp["mb_a_log"][0]), f(inp["mb_d"][0])])[None, :]
    rowp = np.ascontiguousarray(rowp)
    w1 = f(inp["cf_w_pw1"][0])
    nb = 2 * D // 512
    pw1p = np.ascontiguousarray(np.concatenate(
        [np.concatenate([w1[:, 256 * i:256 * i + 256], w1[:, D + 256 * i:D + 256 * i + 256]], axis=1) for i in range(nb)], axis=1))
    w_in = f(inp["mb_w_in"][0])
    DI = c.DI
    w_z = np.ascontiguousarray(w_in[:, :DI]); w_xbc = np.ascontiguousarray(w_in[:, DI:DI + XC * 128])
    w_dt = np.ascontiguousarray(w_in[:, DI + XC * 128:])
    shared = dict(pvec=pv, rowp=rowp, cstf=_consts())
    wAB = dict(w_mod0=f(inp["w_mod"][0]), w_mod1=f(inp["w_mod"][1]), pw1p=pw1p, pw2=f(inp["cf_w_pw2"][0]),
               w1_0=f(inp["mlp_w1"][0]), w2_0=f(inp["mlp_w2"][0]), w_xbc=w_xbc, w_dt=w_dt)
    wC = dict(w_mod1=f(inp["w_mod"][1]), w_xbc=w_xbc, w_z=w_z, w_dt=w_dt, w_out=f(inp["mb_w_out"][0]),
              w1_1=f(inp["mlp_w1"][1]), w2_1=f(inp["mlp_w2"][1]))
    x = f(inp["x"]); B, S, _ = x.shape
    per_seq = S // T
    assert per_seq == c.NQ and B * per_seq == n_cores
    cores = []
    for k in range(n_cores):
        b, r = divmod(k, per_seq)
        xs_ = x[b, r * T:(r + 1) * T, :]
        halo = np.zeros((c.PRE, D), np.float32)
        if r > 0:
            halo[:] = x[b, r * T - c.PRE:r * T, :]
        cores.append(dict(b=b, r=r, x_fm=np.ascontiguousarray(xs_.T), x_halo=np.ascontiguousarray(halo.T),
                          cvec=_fm(inp["c"][b]), hmask=np.full((128, 1), 1.0 if r > 0 else 0.0, np.float32)))
    return shared, wAB, wC, cores


_NC_CACHE = {}


def _get_nc(cfg, phase):
    key = (cfg.D, cfg.G, cfg.T, cfg.NQ, phase)
    if key not in _NC_CACHE:
        _NC_CACHE[key] = build(cfg, phase)
    return _NC_CACHE[key]


def run_module(cfg, inp, n_cores):
    c = cfg
    shared, wAB, wC, cores = host_prep(cfg, inp, n_cores)
    ncA = build(cfg, "AB")
    mapsA = []
    for k in range(n_cores):
        m = dict(shared); m.update(wAB)
        m.update(x_fm=cores[k]["x_fm"], x_halo=cores[k]["x_halo"], cvec=cores[k]["cvec"], hmask=cores[k]["hmask"])
        mapsA.append(m)
    resA = run_bass_kernel_spmd(ncA, mapsA, core_ids=list(range(n_cores))).results
    ncC = build(cfg, "C")
    NQ = c.NQ
    mapsC = []
    for k in range(n_cores):
        b, r = cores[k]["b"], cores[k]["r"]
        grp = [b * NQ + j for j in range(NQ)]
        s_all = np.ascontiguousarray(np.stack([resA[j]["s_loc"] for j in grp], axis=0))
        atot_all = np.ascontiguousarray(np.concatenate([resA[j]["atot"] for j in grp], axis=1))
        inc = np.zeros((NQ, NQ), np.float32); valid = np.zeros((NQ,), np.float32)
        for j in range(NQ):
            valid[j] = 1.0 if j < r else 0.0
            for m_ in range(NQ):
                inc[j, m_] = 1.0 if (j < m_ < r) else 0.0
        m = dict(shared); m.update(wC)
        m.update(h1=resA[k]["h1"], rawhalo=resA[k]["rawhalo"], s_all=s_all, atot_all=atot_all,
                 inc=np.ascontiguousarray(np.broadcast_to(inc.reshape(1, -1), (128, NQ * NQ))),
                 valid=np.ascontiguousarray(np.broadcast_to(valid.reshape(1, -1), (128, NQ))),
                 cvec=cores[k]["cvec"], hmask=cores[k]["hmask"])
        mapsC.append(m)
    resC = run_bass_kernel_spmd(ncC, mapsC, core_ids=list(range(n_cores))).results
    B = n_cores // NQ
    out = np.empty((B, NQ * c.T, c.D), np.float32)
    for k in range(n_cores):
        b, r = cores[k]["b"], cores[k]["r"]
        out[b, r * c.T:(r + 1) * c.T, :] = resC[k]["out_fm"].T
    return out, resA, resC


def kernel(**inputs):
    cfg = Cfg()
    out, _, _ = run_module(cfg, inputs, 8)
    return out
```

```python
import numpy as np
from contextlib import ExitStack
import concourse.bass as bass
import concourse.mybir as mybir
from concourse.bass_utils import run_bass_kernel_spmd

F32 = mybir.dt.float32
BF16 = mybir.dt.bfloat16
AF = mybir.ActivationFunctionType
ALU = mybir.AluOpType
AX = mybir.AxisListType
EPS = 1e-6
SAME_ENGINE_SYNC = True


class Cfg:
    def __init__(s, D=2048, G=8, T=4096, NQ=4, TT=512, TH=256):
        s.D = D; s.G = G; s.T = T; s.NQ = NQ; s.TT = TT; s.TH = TH
        s.DFF = 4 * D; s.DI = 2 * D; s.NH = s.DI // 64; s.NS = 128
        assert s.NH == 8 * G
        s.KC = D // 128; s.FC = s.DFF // 128; s.IC = s.DI // 128
        s.XC = s.IC + 2 * G
        s.CK = 31; s.HW = 32; s.PRE = 64
        s.GW = G * 512
        o = 0
        def alloc(n):
            nonlocal o
            r = o; o += n; return r
        KC, XC, IC = s.KC, s.XC, s.IC
        s.pv = {}
        for nm, n in [("nmix0", KC), ("nmlp0", KC), ("bpa", KC), ("bpg", KC), ("bdw", KC), ("lng", KC),
                      ("lnb", KC), ("bpw2", KC), ("wdw", KC * 31), ("bmod0", 6 * KC),
                      ("nmix1", KC), ("nmlp1", KC), ("cw", XC * 4), ("cb", XC), ("ng", IC),
                      ("fg", KC), ("bmod1", 6 * KC)]:
            s.pv[nm] = (alloc(n), n)
        s.NV = o
        s.NR = 3 * s.NH


class Buf:
    __slots__ = ("w", "r")

    def __init__(s):
        s.w = None; s.r = {}


class TT_:
    def __init__(s, t):
        s.t = t; s.b = Buf()


class Prog:
    def __init__(s, nc, es):
        s.nc = nc; s.es = es
        s.eng = {"pe": nc.tensor, "act": nc.scalar, "dve": nc.vector, "pool": nc.gpsimd, "sp": nc.sync}
        s.sems = {}
        s.cnt = {}
        for k in s.eng:
            s.sems[k] = es.enter_context(nc.semaphore("sem_" + k)); s.cnt[k] = 0
        s.known = {k: {} for k in s.eng}
        s.nwait = 0; s.nins = 0

    def newsem(s, key):
        s.sems[key] = s.es.enter_context(s.nc.semaphore("sem_" + key)); s.cnt[key] = 0
        return key

    def _deps(s, reads, writes):
        deps = []
        for b in reads:
            if b.w: deps.append(b.w)
        for b in writes:
            if b.w: deps.append(b.w)
            deps.extend(b.r.items())
        return deps

    def _wait(s, e, deps):
        need = {}
        for k, v in deps:
            if k == e:
                if v > s.cnt[e]: continue
                if e in ("pe", "sp") or not SAME_ENGINE_SYNC: continue
            if s.known[e].get(k, 0) >= v: continue
            need[k] = max(need.get(k, 0), v)
        for k, v in need.items():
            s.eng[e].wait_ge(s.sems[k], v); s.known[e][k] = v; s.nwait += 1

    def op(s, e, fn, reads=(), writes=(), sig=True):
        s._wait(e, s._deps(reads, writes))
        ins = fn(s.eng[e]); s.nins += 1
        if sig:
            s.cnt[e] += 1; ins.then_inc(s.sems[e], 1); v = s.cnt[e]
        else:
            v = s.cnt[e] + 1
        for b in reads: b.r[e] = max(b.r.get(e, 0), v)
        for b in writes: b.w = (e, v); b.r = {}

    def dma(s, q, semkey, out, in_, reads=(), writes=()):
        s._wait(q, s._deps(reads, writes))
        ins = s.eng[q].dma_start(out=out, in_=in_); s.nins += 1
        s.cnt[semkey] += 16; ins.then_inc(s.sems[semkey], 16); v = s.cnt[semkey]
        for b in reads: b.r[semkey] = max(b.r.get(semkey, 0), v)
        for b in writes: b.w = (semkey, v); b.r = {}

    def finish(s, e, bufs):
        deps = []
        for b in bufs:
            if b.w: deps.append(b.w)
            deps.extend(b.r.items())
        s._wait(e, deps)


def build(cfg, phase):
    c = cfg
    D, KC, FC, IC, XC, G, NH, T, TT, TH, GW = c.D, c.KC, c.FC, c.IC, c.XC, c.G, c.NH, c.T, c.TT, c.TH, c.GW
    NQ = c.NQ
    nc = bass.Bass("TRN2", target_bir_lowering=False)
    dr = {}

    def din(name, shape, dt=F32):
        dr[name] = nc.dram_tensor(name, list(shape), dt, kind="ExternalInput").ap(); return dr[name]

    def dout(name, shape, dt=F32):
        dr[name] = nc.dram_tensor(name, list(shape), dt, kind="ExternalOutput").ap(); return dr[name]

    din("pvec", [128, c.NV]); din("rowp", [1, c.NR]); din("cvec", [128, KC]); din("hmask", [128, 1])
    din("cstf", [128, 4 * 128])
    if phase == "AB":
        din("x_fm", [D, T]); din("x_halo", [D, c.PRE])
        din("w_mod0", [D, 6 * D]); din("w_mod1", [D, 6 * D])
        din("pw1p", [D, 2 * D]); din("pw2", [D, D]); din("w1_0", [D, c.DFF]); din("w2_0", [c.DFF, D])
        din("w_xbc", [D, XC * 128]); din("w_dt", [D, NH])
        dout("h1", [D, T]); dout("s_loc", [128, GW]); dout("atot", [128, NH]); dout("rawhalo", [128, XC * 3])
    else:
        din("h1", [D, T]); din("rawhalo", [128, XC * 3]); din("s_all", [NQ, 128, GW]); din("atot_all", [128, NQ * NH])
        din("inc", [128, NQ * NQ]); din("valid", [128, NQ])
        din("w_mod1", [D, 6 * D]); din("w_xbc", [D, XC * 128]); din("w_z", [D, c.DI]); din("w_dt", [D, NH])
        din("w_out", [c.DI, D]); din("w1_1", [D, c.DFF]); din("w2_1", [c.DFF, D])
        dout("out_fm", [D, T])

    es = ExitStack()
    with es:
        P = Prog(nc, es)
        blk = es.enter_context(nc.Block())

        def sb(name, shape, dt=F32):
            return TT_(es.enter_context(nc.sbuf_tensor(name, list(shape), dt)))

        isAB = (phase == "AB")
        TP = TT if isAB else TH
        if isAB:
            h = sb("h", [128, KC, TP]); hB = [Buf() for _ in range(KC)]
            u = sb("u", [128, KC, TP], BF16); uB = [Buf() for _ in range(KC)]
            BIGE = FC * TT
            NWB = 2; WC = 512
        else:
            h2 = [sb(f"hres{i}", [128, KC, TP]) for i in range(2)]; hB2 = [[Buf() for _ in range(KC)] for _ in range(2)]
            h = h2[0]; hB = hB2[0]
            ua = sb("ua", [128, KC, TP], BF16); uaB = [Buf() for _ in range(KC)]
            ub = sb("ub", [128, KC, TP], BF16); ubB = [Buf() for _ in range(KC)]
            u = ua; uB = uaB
            HIDE = (FC // 2) * TH
            BIGE = HIDE + XC * TH + (TH // 128) * c.DI + IC * TH
            NWB = 3; WC = 256
        big = es.enter_context(nc.sbuf_tensor("big", [128, BIGE], BF16))
        KBS = min(16, KC)
        wbuf = [sb(f"wb{i}", [128, 16, WC], BF16) for i in range(NWB)]
        wsem = [P.newsem(f"w{i}") for i in range(NWB)]
        pvec = sb("pvec_sb", [128, c.NV]); rowp = sb("rowp_sb", [128, c.NR])
        cstf = sb("cstf_sb", [128, 512]); identb = sb("identb", [128, 128], BF16)
        misc = sb("misc", [128, 8]); hmask = sb("hmask_sb", [128, 1])
        modv = [sb(f"mod{i}", [128, 6 * KC]) for i in range(2)]
        der = sb("der", [128, 6 * KC])
        scv = sb("scv", [128, KC], BF16); cvt = sb("cvt", [128, KC])
        rstd = sb("rstd", [128, TP]); tmpA = [sb(f"tmpA{i}", [128, TP + 4]) for i in range(2)]
        sqs = [sb(f"sqs{i}", [128, TP]) for i in range(2)]
        state = sb("state", [128, GW]); stB = [Buf() for _ in range(G)]
        rawhalo = sb("rawhalo_sb", [128, XC, 3]); rhB = [Buf() for _ in range(XC)]
        rawb = tmpA; accb = sqs
        wdt = sb("wdt", [128, KC, NH], BF16)
        sm = {n: sb("sm_" + n, [128, NH]) for n in ["t1", "t2", "dt", "a", "acs", "eac", "d", "ed", "wend", "etot", "atot"]}
        xs = sb("xs", [128, 640], BF16); xw = sb("xw", [128, 512], BF16)
        ld_sem = P.newsem("ld"); st_sem = P.newsem("st"); c_sem = P.newsem("cld"); cw_sem = P.newsem("cwl"); lds_sem = P.newsem("lds"); strh_sem = P.newsem("strh"); sts_sem = P.newsem("sts"); sta_sem = P.newsem("sta"); c2_sem = P.newsem("cld2"); ld2_sem = [ld_sem, P.newsem("ld1")]

        ident_f = cstf.t[:, 0:128]; ones_f = cstf.t[:, 128:256]; triu_f = cstf.t[:, 256:384]; ustr_f = cstf.t[:, 384:512]
        epsT = misc.t[:, 0:1]; oneT = misc.t[:, 1:2]

        def pvc(nm, i=0, n=1):
            o, _ = c.pv[nm]
            return pvec.t[:, o + i:o + i + n]

        if isAB:
            banks = [TT_(es.enter_context(nc.psum_tensor(f"bank{i}", [128, 512], F32))) for i in range(6)]
            segp = TT_(es.enter_context(nc.psum_tensor("segp", [128, 1024], F32)))
            mm = banks[0:4]; aux0 = banks[4]; aux1 = banks[5]
            PJ = {"pairs": [mm[0:4]], "bw": 4}
        else:
            banks = [TT_(es.enter_context(nc.psum_tensor(f"bank{i}", [128, 512], F32))) for i in range(8)]
            mm = banks[0:4]; aux0 = banks[4]; aux1 = banks[5]; MB0 = banks[6]; MB1 = banks[7]
            segp = banks[3]
            PJ = {"pairs": [[banks[0], banks[1]], [banks[2], banks[3]]], "bw": 2}
        newp_bank = mm[2]

        P.dma("sp", c_sem, pvec.t[:], dr["pvec"][:], writes=[pvec.b])
        P.dma("sp", c_sem, rowp.t[:], dr["rowp"][0:1, :].partition_broadcast(128), writes=[rowp.b])
        P.dma("sp", c_sem, cstf.t[:], dr["cstf"][:], writes=[cstf.b])
        P.dma("sp", c_sem, cvt.t[:], dr["cvec"][:], writes=[cvt.b])
        P.dma("sp", c_sem, hmask.t[:], dr["hmask"][:], writes=[hmask.b])
        for tt_ in (pvec, rowp, cstf, cvt, hmask):
            tt_.b.w = (c_sem, P.cnt[c_sem])
        P.op("dve", lambda e: e.memset(misc.t[:, 0:1], EPS), writes=[misc.b])
        P.op("dve", lambda e: e.memset(misc.t[:, 1:2], 1.0), writes=[misc.b])
        P.op("dve", lambda e: e.tensor_copy(out=identb.t[:], in_=ident_f), reads=[cstf.b], writes=[identb.b])
        P.op("act", lambda e: e.activation(out=scv.t[:], in_=cvt.t[:], func=AF.Silu), reads=[cvt.b], writes=[scv.b])
        P.op("act", lambda e: e.activation(out=rowp.t[:, NH:2 * NH], in_=rowp.t[:, NH:2 * NH], func=AF.Exp),
             reads=[rowp.b], writes=[rowp.b])
        P.op("dve", lambda e: e.tensor_scalar(out=rowp.t[:, NH:2 * NH], in0=rowp.t[:, NH:2 * NH], scalar1=-1.0,
                                              scalar2=None, op0=ALU.mult), reads=[rowp.b], writes=[rowp.b])
        dtb_bc = rowp.t[:, 0:NH]; aneg_bc = rowp.t[:, NH:2 * NH]; dvec_bc = rowp.t[:, 2 * NH:3 * NH]

        wcast = {}

        def precast(name):
            src = dr[name]
            dst = nc.dram_tensor(name + "_bf", list(src.shape), BF16, kind="Internal").ap()
            b = Buf(); sk = P.newsem("pc_" + name)
            R = src.shape[0]
            step = 2048
            for r0 in range(0, R, step):
                r1 = min(R, r0 + step)
                P.dma("pool", sk, dst[r0:r1, :], src[r0:r1, :], writes=[])
            b.w = (sk, P.cnt[sk])
            wcast[name] = (dst, b)

        wstate = {"i": 0}

        def wload(src, k0c, nkc, n0, ncols):
            i = wstate["i"] % NWB; wstate["i"] += 1
            wb = wbuf[i]
            rd = []
            if isinstance(src, str):
                src, cb_ = wcast[src]; rd = [cb_]
            P.dma("pool", wsem[i], wb.t[:, 0:nkc, 0:ncols],
                  src[k0c * 128:(k0c + nkc) * 128, n0:n0 + ncols].rearrange("(kc p) n -> p kc n", p=128),
                  reads=rd, writes=[wb.b])
            return wb

        def proj_gen(src, K_chunks, ncols_total, rhs_fn, NT, evac, pairs, bw):
            nkb = (K_chunks + KBS - 1) // KBS
            for nb in range(ncols_total // (bw * 128)):
                bk = pairs[nb % len(pairs)]
                for kb in range(nkb):
                    nkc = min(KBS, K_chunks - kb * KBS)
                    wb = wload(src, kb * KBS, nkc, nb * bw * 128, bw * 128)
                    for j in range(bw):
                        for kc in range(nkc):
                            ap, rb = rhs_fn(kb * KBS + kc)
                            first = (kb == 0 and kc == 0); last = (kb == nkb - 1 and kc == nkc - 1)
                            P.op("pe", lambda e: e.matmul(out=bk[j].t[:, :NT], lhsT=wb.t[:, kc, j * 128:(j + 1) * 128],
                                                          rhs=ap, start=first, stop=last),
                                 reads=[wb.b, rb], writes=[bk[j].b], sig=(last or (j == bw - 1 and kc == nkc - 1)))
                    if kb == nkb - 1:
                        for j in range(bw):
                            evac(nb * bw + j, bk[j], nb, j)
                    yield

        def proj_A(src, K_chunks, ncols_total, rhs_fn, NT, evac):
            for _ in proj_gen(src, K_chunks, ncols_total, rhs_fn, NT, evac, PJ["pairs"], PJ["bw"]):
                pass

        def compute_mod(layer, wsrc):
            mv = modv[layer]
            NJ = WC // 128
            for nb in range(6 * D // WC):
                wb = wload(wsrc, 0, KC, nb * WC, WC)
                for j in range(NJ):
                    for kc in range(KC):
                        P.op("pe", lambda e: e.matmul(out=aux0.t[:, j:j + 1], lhsT=wb.t[:, kc, j * 128:(j + 1) * 128],
                                                      rhs=scv.t[:, kc:kc + 1], start=(kc == 0), stop=(kc == KC - 1)),
                             reads=[wb.b, scv.b], writes=[aux0.b], sig=(kc == KC - 1))
                o, _ = c.pv["bmod%d" % layer]
                P.op("dve", lambda e: e.tensor_tensor(out=mv.t[:, nb * NJ:nb * NJ + NJ], in0=aux0.t[:, 0:NJ],
                                                      in1=pvec.t[:, o + nb * NJ:o + nb * NJ + NJ], op=ALU.add),
                     reads=[aux0.b, pvec.b], writes=[mv.b])

        def derive(layer):
            mv = modv[layer]
            for (dst, nm, sc0) in [(0, "nmix%d" % layer, KC), (KC, "nmlp%d" % layer, 4 * KC)]:
                P.op("dve", lambda e: e.scalar_tensor_tensor(out=der.t[:, dst:dst + KC], in0=mv.t[:, sc0:sc0 + KC], scalar=1.0,
                                                             in1=pvc(nm, 0, KC), op0=ALU.add, op1=ALU.mult),
                     reads=[mv.b, pvec.b], writes=[der.b])
            if layer == 0:
                P.op("dve", lambda e: e.tensor_tensor(out=der.t[:, 2 * KC:3 * KC], in0=mv.t[:, 2 * KC:3 * KC],
                                                      in1=pvc("bpw2", 0, KC), op=ALU.mult),
                     reads=[mv.b, pvec.b], writes=[der.b])

        def rmsnorm_to(NT, src_fn, scale_fn, bias_fn, dst_fn, t0=0):
            for cc in range(KC):
                sq = sqs[cc % 2]
                ap, rb = src_fn(cc)
                P.op("act", lambda e: e.activation(out=sq.t[:, :NT], in_=ap, func=AF.Square), reads=[rb], writes=[sq.b])
                P.op("pe", lambda e: e.matmul(out=aux0.t[:, :NT], lhsT=ones_f, rhs=sq.t[:, :NT], start=(cc == 0),
                                              stop=(cc == KC - 1)), reads=[sq.b, cstf.b], writes=[aux0.b])
            P.op("act", lambda e: e.activation(out=rstd.t[:, :NT], in_=aux0.t[:, :NT], func=AF.Sqrt, scale=1.0 / D, bias=epsT),
                 reads=[aux0.b, misc.b], writes=[rstd.b])
            P.op("dve", lambda e: e.reciprocal(out=rstd.t[:, :NT], in_=rstd.t[:, :NT]), reads=[rstd.b], writes=[rstd.b])
            for cc in range(KC):
                tp = tmpA[cc % 2]
                ap, rb = src_fn(cc)
                P.op("dve", lambda e: e.tensor_tensor(out=tp.t[:, :NT], in0=ap, in1=rstd.t[:, :NT], op=ALU.mult),
                     reads=[rb, rstd.b], writes=[tp.b])
                dap, db = dst_fn(cc)
                sc_ap, sc_b = scale_fn(cc)
                if bias_fn is None:
                    P.op("act", lambda e: e.activation(out=dap, in_=tp.t[:, :NT], func=AF.Copy, scale=sc_ap),
                         reads=[tp.b, sc_b], writes=[db])
                else:
                    bi_ap, bi_b = bias_fn(cc)
                    P.op("act", lambda e: e.activation(out=dap, in_=tp.t[:, :NT], func=AF.Identity, scale=sc_ap, bias=bi_ap),
                         reads=[tp.b, sc_b, bi_b], writes=[db])

        def h_src(NT, t0=0):
            return lambda cc: (h.t[:, cc, t0:t0 + NT], hB[cc])

        def u_dst(NT, t0=0):
            return lambda cc: (u.t[:, cc, t0:t0 + NT], uB[cc])

        def u_rhs(NT, t0=0):
            return lambda kc: (u.t[:, kc, t0:t0 + NT], uB[kc])

        def mlp(NT, layer, w1src, w2src):
            mv = modv[layer]
            rmsnorm_to(NT, h_src(NT), lambda cc: (der.t[:, KC + cc:KC + cc + 1], der.b),
                       lambda cc: (mv.t[:, 3 * KC + cc:3 * KC + cc + 1], mv.b), u_dst(NT))
            hid = big[:, 0:FC * TT].rearrange("p (c t) -> p c t", t=TT)

            def ev1(fc, bank, nb, j):
                sq = sqs[fc % 2]
                P.op("act", lambda e: e.activation(out=sq.t[:, :NT], in_=bank.t[:, :NT], func=AF.Square),
                     reads=[bank.b], writes=[sq.b])
                P.op("dve", lambda e: e.scalar_tensor_tensor(out=hid[:, fc, :NT], in0=bank.t[:, :NT], scalar=0.0,
                                                             in1=sq.t[:, :NT], op0=ALU.is_gt, op1=ALU.mult),
                     reads=[bank.b, sq.b], writes=[bigB[fc]])
            proj_A(w1src, KC, c.DFF, u_rhs(NT), NT, ev1)

            def ev2(dc, bank, nb, j):
                P.op("dve", lambda e: e.scalar_tensor_tensor(out=h.t[:, dc, :NT], in0=bank.t[:, :NT],
                                                             scalar=mv.t[:, 5 * KC + dc:5 * KC + dc + 1],
                                                             in1=h.t[:, dc, :NT], op0=ALU.mult, op1=ALU.add),
                     reads=[bank.b, mv.b, hB[dc]], writes=[hB[dc]])
            proj_A(w2src, FC, D, lambda kc: (hid[:, kc, :NT], bigB[kc]), NT, ev2)

        bigB = [Buf() for _ in range(max(FC, 64))]

        XS = TT if phase == "AB" else TH
        XOFF = 0 if isAB else HIDE
        xbcT = big[:, XOFF:XOFF + XC * XS].rearrange("p (c t) -> p c t", t=XS)
        ZOFF = XOFF + XC * XS
        xbcB = [Buf() for _ in range(XC)] if not isAB else None
        XB = bigB if isAB else xbcB
        NCH_MAX = TT // 128

        def mamba_inproj_xbc(NT, t0, nchunks_xbc):
            def ev(cc, bank, nb, j):
                rw = rawb[cc % 2]; ac = accb[cc % 2]
                P.op("act", lambda e: e.activation(out=rw.t[:, 0:3], in_=rawhalo.t[:, cc, :], func=AF.Copy),
                     reads=[rhB[cc]], writes=[rw.b])
                P.op("act", lambda e: e.activation(out=rw.t[:, 3:3 + NT], in_=bank.t[:, :NT], func=AF.Copy),
                     reads=[bank.b], writes=[rw.b])
                P.op("act", lambda e: e.activation(out=rawhalo.t[:, cc, :], in_=rw.t[:, NT:NT + 3], func=AF.Copy),
                     reads=[rw.b], writes=[rhB[cc]])
                P.op("dve", lambda e: e.tensor_scalar(out=ac.t[:, :NT], in0=rw.t[:, 0:NT], scalar1=pvc("cw", cc * 4 + 0),
                                                      scalar2=pvc("cb", cc), op0=ALU.mult, op1=ALU.add),
                     reads=[rw.b, pvec.b], writes=[ac.b])
                for k in range(1, 4):
                    P.op("dve", lambda e: e.scalar_tensor_tensor(out=ac.t[:, :NT], in0=rw.t[:, k:k + NT],
                                                                 scalar=pvc("cw", cc * 4 + k), in1=ac.t[:, :NT],
                                                                 op0=ALU.mult, op1=ALU.add),
                         reads=[rw.b, pvec.b, ac.b], writes=[ac.b])
                P.op("act", lambda e: e.activation(out=xbcT[:, cc, :NT], in_=ac.t[:, :NT], func=AF.Silu),
                     reads=[ac.b], writes=[XB[cc]])
            proj_A("w_xbc", KC, ((nchunks_xbc + 3) // 4) * 512, u_rhs(NT, t0), NT, ev)

        def ssd_dt(t0, tc):
            for kc in range(KC):
                P.op("pe", lambda e: e.matmul(out=aux0.t[:, 0:NH], lhsT=u.t[:, kc, t0 + tc * 128:t0 + (tc + 1) * 128],
                                              rhs=wdt.t[:, kc, :], start=(kc == 0), stop=(kc == KC - 1)),
                     reads=[uB[kc], wdt.b], writes=[aux0.b], sig=(kc == KC - 1))
            S = sm
            P.op("dve", lambda e: e.tensor_tensor(out=S["t1"].t[:], in0=aux0.t[:, 0:NH], in1=dtb_bc, op=ALU.add),
                 reads=[aux0.b, rowp.b], writes=[S["t1"].b])
            P.op("act", lambda e: e.activation(out=S["t2"].t[:], in_=S["t1"].t[:], func=AF.Exp), reads=[S["t1"].b], writes=[S["t2"].b])
            P.op("act", lambda e: e.activation(out=S["dt"].t[:], in_=S["t2"].t[:], func=AF.Ln, bias=oneT),
                 reads=[S["t2"].b, misc.b], writes=[S["dt"].b])
            P.op("dve", lambda e: e.tensor_tensor(out=S["a"].t[:], in0=S["dt"].t[:], in1=aneg_bc, op=ALU.mult),
                 reads=[S["dt"].b, rowp.b], writes=[S["a"].b])
            P.op("pe", lambda e: e.matmul(out=aux0.t[:, 64:64 + NH], lhsT=triu_f, rhs=S["a"].t[:], start=True, stop=True),
                 reads=[S["a"].b, cstf.b], writes=[aux0.b])
            P.op("pe", lambda e: e.matmul(out=aux0.t[:, 128:128 + NH], lhsT=ones_f, rhs=S["a"].t[:], start=True, stop=True),
                 reads=[S["a"].b, cstf.b], writes=[aux0.b])
            P.op("act", lambda e: e.activation(out=S["acs"].t[:], in_=aux0.t[:, 64:64 + NH], func=AF.Copy),
                 reads=[aux0.b], writes=[S["acs"].b])
            P.op("act", lambda e: e.activation(out=S["eac"].t[:], in_=aux0.t[:, 64:64 + NH], func=AF.Exp),
                 reads=[aux0.b], writes=[S["eac"].b])
            P.op("dve", lambda e: e.tensor_tensor(out=S["d"].t[:], in0=aux0.t[:, 128:128 + NH], in1=S["acs"].t[:], op=ALU.subtract),
                 reads=[aux0.b, S["acs"].b], writes=[S["d"].b])
            P.op("act", lambda e: e.activation(out=S["ed"].t[:], in_=S["d"].t[:], func=AF.Exp), reads=[S["d"].b], writes=[S["ed"].b])
            P.op("dve", lambda e: e.tensor_tensor(out=S["wend"].t[:], in0=S["ed"].t[:], in1=S["dt"].t[:], op=ALU.mult),
                 reads=[S["ed"].b, S["dt"].b], writes=[S["wend"].b])
            P.op("act", lambda e: e.activation(out=S["etot"].t[:], in_=aux0.t[:, 128:128 + NH], func=AF.Exp),
                 reads=[aux0.b], writes=[S["etot"].b])
            P.op("dve", lambda e: e.tensor_tensor(out=S["atot"].t[:], in0=S["atot"].t[:], in1=aux0.t[:, 128:128 + NH], op=ALU.add),
                 reads=[aux0.b, S["atot"].b], writes=[S["atot"].b])

        aux1b = aux1.t[:].bitcast(BF16)

        def ssd_tokmajor(g, tc):
            for j in range(4):
                P.op("pe", lambda e: e.transpose(out=aux1b[:, j * 128:(j + 1) * 128],
                                                 in_=xbcT[:, g * 4 + j, tc * 128:(tc + 1) * 128], identity=identb.t[:]),
                     reads=[XB[g * 4 + j], identb.b], writes=[aux1.b], sig=False)
            P.op("pe", lambda e: e.transpose(out=aux1b[:, 512:640], in_=xbcT[:, IC + g, tc * 128:(tc + 1) * 128],
                                             identity=identb.t[:]), reads=[XB[IC + g], identb.b], writes=[aux1.b])
            P.op("act", lambda e: e.activation(out=xs.t[:, 0:640], in_=aux1b[:, 0:640], func=AF.Copy),
                 reads=[aux1.b], writes=[xs.b])

        def bc8(ap, n):
            return ap.unsqueeze(2).to_broadcast([128, 8, n])

        def ssd_state_update(g):
            S = sm
            P.op("dve", lambda e: e.tensor_tensor(out=xw.t[:].rearrange("p (h q) -> p h q", q=64),
                                                  in0=xs.t[:, 0:512].rearrange("p (h q) -> p h q", q=64),
                                                  in1=bc8(S["wend"].t[:, g * 8:(g + 1) * 8], 64), op=ALU.mult),
                 reads=[xs.b, S["wend"].b], writes=[xw.b])
            P.op("pe", lambda e: e.matmul(out=newp_bank.t[:, :], lhsT=xs.t[:, 512:640], rhs=xw.t[:], start=True, stop=True),
                 reads=[xs.b, xw.b], writes=[newp_bank.b])
            stg = state.t[:, g * 512:(g + 1) * 512]
            P.op("dve", lambda e: e.tensor_tensor(out=stg.rearrange("p (h q) -> p h q", q=64),
                                                  in0=stg.rearrange("p (h q) -> p h q", q=64),
                                                  in1=bc8(S["etot"].t[:, g * 8:(g + 1) * 8], 64), op=ALU.mult),
                 reads=[stB[g], S["etot"].b], writes=[stB[g]])
            P.op("dve", lambda e: e.tensor_tensor(out=stg, in0=stg, in1=newp_bank.t[:, :], op=ALU.add),
                 reads=[stB[g], newp_bank.b], writes=[stB[g]])

        if phase == "AB":
            vpad = big[:, 0:KC * (c.HW + TT)].rearrange("p (c t) -> p c t", t=c.HW + TT)
            VB = [Buf() for _ in range(KC)]
            CVO_OFF = KC * (c.HW + TT)
            CVO_OFF += CVO_OFF % 2
            cvo = big[:, CVO_OFF:CVO_OFF + 2 * KC * TT].bitcast(F32).rearrange("p (c t) -> p c t", t=TT)
            CB = [Buf() for _ in range(KC)]
            dgs = [sb(f"dg{i}", [128, 16, 128], BF16) for i in range(2)]
            mean_t = sb("mean_t", [128, TT]); msq = sb("msq", [128, TT])
            P.dma("pool", cw_sem, wdt.t[:], dr["w_dt"].rearrange("(kc p) n -> p kc n", p=128), writes=[wdt.b])
            for nm_ in ["pw1p", "pw2", "w1_0", "w2_0", "w_xbc"]:
                precast(nm_)
            compute_mod(0, dr["w_mod0"]); compute_mod(1, dr["w_mod1"])
            P.op("dve", lambda e: e.memset(state.t[:], 0.0), writes=stB)
            P.op("dve", lambda e: e.memset(sm["atot"].t[:], 0.0), writes=[sm["atot"].b])
            P.op("dve", lambda e: e.memset(rawhalo.t[:], 0.0), writes=rhB)
            P.op("dve", lambda e: e.memset(vpad[:, :, 0:c.HW], 0.0), writes=VB)

            def layer0(NT):
                mv = modv[0]
                rmsnorm_to(NT, h_src(NT), lambda cc: (der.t[:, cc:cc + 1], der.b),
                           lambda cc: (mv.t[:, cc:cc + 1], mv.b), u_dst(NT))

                def evg(ci, bank, nb, j):
                    if j < 2: return
                    jj = j - 2; cc = nb * 2 + jj
                    sg = tmpA[cc % 2]
                    P.op("act", lambda e: e.activation(out=sg.t[:, :NT], in_=bank.t[:, :NT], func=AF.Sigmoid, bias=pvc("bpg", cc)),
                         reads=[bank.b, pvec.b], writes=[sg.b])
                    P.op("dve", lambda e: e.scalar_tensor_tensor(out=vpad[:, cc, c.HW:c.HW + NT], in0=mm[jj].t[:, :NT],
                                                                 scalar=pvc("bpa", cc), in1=sg.t[:, :NT],
                                                                 op0=ALU.add, op1=ALU.mult),
                         reads=[mm[jj].b, pvec.b, sg.b], writes=[VB[cc]])
                proj_A("pw1p", KC, 2 * D, u_rhs(NT), NT, evg)

                for cc in range(KC):
                    o, _ = c.pv["wdw"]
                    bank = mm[cc % 4]
                    for hf, (k0, k1) in enumerate([(0, 16), (16, 31)]):
                        dg = dgs[hf]; nk = k1 - k0
                        P.op("dve", lambda e: e.tensor_tensor(out=dg.t[:, 0:nk, :], in0=ident_f.unsqueeze(1).to_broadcast([128, nk, 128]),
                                                              in1=pvec.t[:, o + cc * 31 + k0:o + cc * 31 + k1].unsqueeze(2).to_broadcast([128, nk, 128]),
                                                              op=ALU.mult), reads=[cstf.b, pvec.b], writes=[dg.b])
                        for k in range(k0, k1):
                            P.op("pe", lambda e: e.matmul(out=bank.t[:, :NT], lhsT=dg.t[:, k - k0, :],
                                                          rhs=vpad[:, cc, c.HW - 30 + k:c.HW - 30 + k + NT], start=(k == 0), stop=(k == 30)),
                                 reads=[dg.b, VB[cc]], writes=[bank.b], sig=(k == k1 - 1))
                    sq = sqs[cc % 2]
                    P.op("act", lambda e: e.activation(out=cvo[:, cc, :NT], in_=bank.t[:, :NT], func=AF.Identity, bias=pvc("bdw", cc)),
                         reads=[bank.b, pvec.b], writes=[CB[cc]])
                    P.op("act", lambda e: e.activation(out=sq.t[:, :NT], in_=bank.t[:, :NT], func=AF.Square, bias=pvc("bdw", cc)),
                         reads=[bank.b, pvec.b], writes=[sq.b])
                    P.op("pe", lambda e: e.matmul(out=aux0.t[:, :NT], lhsT=ones_f, rhs=cvo[:, cc, :NT], start=(cc == 0), stop=(cc == KC - 1)),
                         reads=[CB[cc], cstf.b], writes=[aux0.b])
                    P.op("pe", lambda e: e.matmul(out=aux1.t[:, :NT], lhsT=ones_f, rhs=sq.t[:, :NT], start=(cc == 0), stop=(cc == KC - 1)),
                         reads=[sq.b, cstf.b], writes=[aux1.b])
                P.op("dve", lambda e: e.tensor_scalar(out=mean_t.t[:, :NT], in0=aux0.t[:, :NT], scalar1=1.0 / D, scalar2=None, op0=ALU.mult),
                     reads=[aux0.b], writes=[mean_t.b])
                P.op("dve", lambda e: e.tensor_tensor(out=msq.t[:, :NT], in0=mean_t.t[:, :NT], in1=mean_t.t[:, :NT], op=ALU.mult),
                     reads=[mean_t.b], writes=[msq.b])
                P.op("dve", lambda e: e.scalar_tensor_tensor(out=msq.t[:, :NT], in0=aux1.t[:, :NT], scalar=1.0 / D, in1=msq.t[:, :NT],
                                                             op0=ALU.mult, op1=ALU.subtract), reads=[aux1.b, msq.b], writes=[msq.b])
                P.op("act", lambda e: e.activation(out=rstd.t[:, :NT], in_=msq.t[:, :NT], func=AF.Sqrt, bias=epsT),
                     reads=[msq.b, misc.b], writes=[rstd.b])
                P.op("dve", lambda e: e.reciprocal(out=rstd.t[:, :NT], in_=rstd.t[:, :NT]), reads=[rstd.b], writes=[rstd.b])
                for cc in range(KC):
                    tp = tmpA[cc % 2]
                    P.op("dve", lambda e: e.tensor_tensor(out=tp.t[:, :NT], in0=cvo[:, cc, :NT], in1=mean_t.t[:, :NT], op=ALU.subtract),
                         reads=[CB[cc], mean_t.b], writes=[tp.b])
                    P.op("dve", lambda e: e.tensor_tensor(out=tp.t[:, :NT], in0=tp.t[:, :NT], in1=rstd.t[:, :NT], op=ALU.mult),
                         reads=[tp.b, rstd.b], writes=[tp.b])
                    P.op("act", lambda e: e.activation(out=u.t[:, cc, :NT], in_=tp.t[:, :NT], func=AF.Silu, scale=pvc("lng", cc), bias=pvc("lnb", cc)),
                         reads=[tp.b, pvec.b], writes=[uB[cc]])

                def ev2(dc, bank, nb, j):
                    tp = tmpA[dc % 2]
                    P.op("act", lambda e: e.activation(out=tp.t[:, :NT], in_=bank.t[:, :NT], func=AF.Identity,
                                                       scale=mv.t[:, 2 * KC + dc:2 * KC + dc + 1], bias=der.t[:, 2 * KC + dc:2 * KC + dc + 1]),
                         reads=[bank.b, mv.b, der.b], writes=[tp.b])
                    P.op("dve", lambda e: e.tensor_tensor(out=h.t[:, dc, :NT], in0=h.t[:, dc, :NT], in1=tp.t[:, :NT], op=ALU.add),
                         reads=[tp.b, hB[dc]], writes=[hB[dc]])
                proj_A("pw2", KC, D, u_rhs(NT), NT, ev2)

            def save_vhalo(NT, masked):
                for cc in range(KC):
                    if masked:
                        P.op("dve", lambda e: e.tensor_scalar(out=vpad[:, cc, 0:c.HW], in0=vpad[:, cc, NT:NT + c.HW], scalar1=hmask.t[:, 0:1],
                                                              scalar2=None, op0=ALU.mult), reads=[VB[cc], hmask.b], writes=[VB[cc]])
                    else:
                        P.op("dve", lambda e: e.tensor_copy(out=vpad[:, cc, 0:c.HW], in_=vpad[:, cc, NT:NT + c.HW]),
                             reads=[VB[cc]], writes=[VB[cc]])

            vh = sb("vh", [128, KC, c.HW], BF16)

            tiles = [("pre", 0, c.PRE)] + [("main", i * TT, TT) for i in range(T // TT)]
            for (kind, t0, NT) in tiles:
                src = dr["x_halo"] if kind == "pre" else dr["x_fm"]
                P.dma("sp", ld_sem, h.t[:, :, 0:NT], src[:, t0:t0 + NT].rearrange("(c p) t -> p c t", p=128), writes=hB)
                derive(0)
                if kind == "pre":
                    P.op("dve", lambda e: e.memset(vpad[:, :, 0:c.HW], 0.0), writes=VB)
                else:
                    P.op("dve", lambda e: e.tensor_copy(out=vpad[:, :, 0:c.HW], in_=vh.t[:]), reads=[vh.b], writes=VB)
                layer0(NT)
                if kind == "pre":
                    P.op("dve", lambda e: e.tensor_scalar(out=vh.t[:], in0=vpad[:, :, NT:NT + c.HW], scalar1=hmask.t[:, 0:1], scalar2=None,
                                                          op0=ALU.mult), reads=VB + [hmask.b], writes=[vh.b])
                else:
                    P.op("dve", lambda e: e.tensor_copy(out=vh.t[:], in_=vpad[:, :, NT:NT + c.HW]), reads=VB, writes=[vh.b])
                mlp(NT, 0, "w1_0", "w2_0")
                if kind == "main":
                    P.dma("sp", st_sem, dr["h1"][:, t0:t0 + NT].rearrange("(c p) t -> p c t", p=128), h.t[:, :, 0:NT], reads=hB)
                derive(1)
                mv = modv[1]
                rmsnorm_to(NT, h_src(NT), lambda cc: (der.t[:, cc:cc + 1], der.b), lambda cc: (mv.t[:, cc:cc + 1], mv.b), u_dst(NT))
                if kind == "pre":
                    mamba_inproj_xbc(NT, 0, XC)
                    for cc in range(XC):
                        P.op("dve", lambda e: e.tensor_scalar(out=rawhalo.t[:, cc, :], in0=rawhalo.t[:, cc, :], scalar1=hmask.t[:, 0:1],
                                                              scalar2=None, op0=ALU.mult), reads=[rhB[cc], hmask.b], writes=[rhB[cc]])
                    P.dma("sp", strh_sem, dr["rawhalo"][:], rawhalo.t[:].rearrange("p c k -> p (c k)"), reads=rhB)
                else:
                    mamba_inproj_xbc(NT, 0, IC + G)
                    for tc in range(NT // 128):
                        ssd_dt(0, tc)
                        for g in range(G):
                            ssd_tokmajor(g, tc)
                            ssd_state_update(g)
            P.dma("sp", sts_sem, dr["s_loc"][:], state.t[:], reads=stB)
            P.dma("sp", sta_sem, dr["atot"][:], sm["atot"].t[:], reads=[sm["atot"].b])
            P.finish("sp", hB + stB + [sm["atot"].b] + rhB)
            P.eng["sp"].wait_ge(P.sems[st_sem], P.cnt[st_sem])

        else:
            NCH = TH // 128
            HH = FC // 2
            hid = big[:, 0:HIDE].rearrange("p (c t) -> p c t", t=TH)
            hidB = [Buf() for _ in range(HH)]
            ost = big[:, 0:2 * KC * TH].bitcast(F32).rearrange("p (c t) -> p c t", t=TH)
            assert 2 * KC * TH <= HIDE
            zs = big[:, ZOFF:ZOFF + NCH * c.DI].rearrange("p (t c) -> p t c", c=c.DI)
            ZB = [Buf() for _ in range(NCH)]
            YOFF = ZOFF + NCH * c.DI
            assert YOFF + IC * TH <= BIGE
            ynT = TT_(big[:, YOFF:YOFF + IC * TH].rearrange("p (c t) -> p c t", t=TH)); YB = [Buf() for _ in range(IC)]
            stbf = sb("stbf", [128, 512], BF16)
            Lb = sb("Lb", [128, 4, 128]); dec = sb("dec", [128, 4, 128])
            scT = sb("scT", [128, 8, 128], BF16); cbm = sb("cbm", [128, 128])
            yb = sb("yb", [128, 512]); yt = sb("yt", [128, 512]); ssq = sb("ssq", [128, 2])
            coef = sb("coef", [128, NH]); cin = sb("cin", [128, NQ * NQ + NQ]); atl = sb("atl", [128, NQ * NH])
            assert 2 * GW <= XC * TH
            stmp = TT_(big[:, XOFF:XOFF + 2 * GW].bitcast(F32))
            Y0, Y1, Y2, SG = banks[0], banks[1], banks[2], banks[3]
            newp_bank = Y2
            P.dma("pool", cw_sem, wdt.t[:], dr["w_dt"].rearrange("(kc p) n -> p kc n", p=128), writes=[wdt.b])
            P.dma("sp", c2_sem, cin.t[:, 0:NQ * NQ], dr["inc"][:], writes=[cin.b])
            P.dma("sp", c2_sem, cin.t[:, NQ * NQ:], dr["valid"][:], writes=[cin.b])
            P.dma("sp", c2_sem, atl.t[:], dr["atot_all"][:], writes=[atl.b])
            P.dma("sp", c2_sem, rawhalo.t[:].rearrange("p c k -> p (c k)"), dr["rawhalo"][:], writes=rhB)
            for b_ in [cin.b, atl.b] + rhB:
                b_.w = (c2_sem, P.cnt[c2_sem])
            for nm_ in ["w_xbc", "w_z", "w_out", "w1_1", "w2_1"]:
                precast(nm_)
            compute_mod(1, dr["w_mod1"])
            derive(1)
            P.op("dve", lambda e: e.memset(sm["atot"].t[:], 0.0), writes=[sm["atot"].b])
            P.op("dve", lambda e: e.memset(state.t[:], 0.0), writes=stB)
            for j in range(NQ):
                P.op("dve", lambda e: e.tensor_scalar(out=coef.t[:], in0=atl.t[:, 0:NH], scalar1=cin.t[:, j * NQ:j * NQ + 1], scalar2=None,
                                                      op0=ALU.mult), reads=[atl.b, cin.b], writes=[coef.b])
                for m in range(1, NQ):
                    P.op("dve", lambda e: e.scalar_tensor_tensor(out=coef.t[:], in0=atl.t[:, m * NH:(m + 1) * NH],
                                                                 scalar=cin.t[:, j * NQ + m:j * NQ + m + 1], in1=coef.t[:],
                                                                 op0=ALU.mult, op1=ALU.add), reads=[atl.b, cin.b, coef.b], writes=[coef.b])
                P.op("act", lambda e: e.activation(out=coef.t[:], in_=coef.t[:], func=AF.Exp), reads=[coef.b], writes=[coef.b])
                P.op("dve", lambda e: e.tensor_scalar(out=coef.t[:], in0=coef.t[:], scalar1=cin.t[:, NQ * NQ + j:NQ * NQ + j + 1], scalar2=None,
                                                      op0=ALU.mult), reads=[coef.b, cin.b], writes=[coef.b])
                P.dma("sp", lds_sem, stmp.t[:], dr["s_all"][j], writes=[stmp.b])
                P.op("dve", lambda e: e.tensor_tensor(out=stmp.t[:].rearrange("p (h q) -> p h q", q=64),
                                                      in0=stmp.t[:].rearrange("p (h q) -> p h q", q=64),
                                                      in1=coef.t[:].unsqueeze(2).to_broadcast([128, NH, 64]), op=ALU.mult),
                     reads=[stmp.b, coef.b], writes=[stmp.b])
                P.op("dve", lambda e: e.tensor_tensor(out=state.t[:], in0=state.t[:], in1=stmp.t[:], op=ALU.add),
                     reads=[stmp.b] + stB, writes=stB)
            for b_ in XB:
                b_.r = dict(stmp.b.r); b_.w = stmp.b.w

            mv = modv[1]
            MLPP = [[MB0, MB1]]

            def mlp_gen(hp, hpB, t0):
                rmsnorm_to(TH, lambda cc: (hp.t[:, cc, :], hpB[cc]), lambda cc: (der.t[:, KC + cc:KC + cc + 1], der.b),
                           lambda cc: (mv.t[:, 3 * KC + cc:3 * KC + cc + 1], mv.b), lambda cc: (ub.t[:, cc, :], ubB[cc]))
                yield
                for half in range(2):
                    def ev1(fc, bank, nb, j):
                        lc = fc - half * HH
                        sq = sqs[fc % 2]
                        P.op("act", lambda e: e.activation(out=sq.t[:, :TH], in_=bank.t[:, :TH], func=AF.Copy),
                             reads=[bank.b], writes=[sq.b])
                        P.op("dve", lambda e: e.scalar_tensor_tensor(out=hid[:, lc, :], in0=sq.t[:, :TH], scalar=0.0,
                                                                     in1=sq.t[:, :TH], op0=ALU.max, op1=ALU.mult),
                             reads=[sq.b], writes=[hidB[lc]])

                    def w1gen():
                        nkb = 1
                        for nb in range(HH // 2):
                            bk = MLPP[0]
                            wb = wload("w1_1", 0, KC, (half * HH + nb * 2) * 128, 256)
                            for j in range(2):
                                for kc in range(KC):
                                    P.op("pe", lambda e: e.matmul(out=bk[j].t[:, :TH], lhsT=wb.t[:, kc, j * 128:(j + 1) * 128],
                                                                  rhs=ub.t[:, kc, :], start=(kc == 0), stop=(kc == KC - 1)),
                                         reads=[wb.b, ubB[kc]], writes=[bk[j].b], sig=(kc == KC - 1))
                            for j in range(2):
                                ev1(half * HH + nb * 2 + j, bk[j], nb, j)
                            yield
                    yield from w1gen()

                    def ev2(dc, bank, nb, j):
                        P.op("dve", lambda e: e.scalar_tensor_tensor(out=hp.t[:, dc, :], in0=bank.t[:, :TH],
                                                                     scalar=mv.t[:, 5 * KC + dc:5 * KC + dc + 1],
                                                                     in1=hp.t[:, dc, :], op0=ALU.mult, op1=ALU.add),
                             reads=[bank.b, mv.b, hpB[dc]], writes=[hpB[dc]])
                    nkb = (HH + KBS - 1) // KBS
                    for nb in range(D // 256):
                        bk = MLPP[0]
                        for kb in range(nkb):
                            nkc = min(KBS, HH - kb * KBS)
                            wb = wload("w2_1", half * HH + kb * KBS, nkc, nb * 256, 256)
                            for j in range(2):
                                for kc in range(nkc):
                                    first = (kb == 0 and kc == 0); last = (kb == nkb - 1 and kc == nkc - 1)
                                    P.op("pe", lambda e: e.matmul(out=bk[j].t[:, :TH], lhsT=wb.t[:, kc, j * 128:(j + 1) * 128],
                                                                  rhs=hid[:, kb * KBS + kc, :], start=first, stop=last),
                                         reads=[wb.b, hidB[kb * KBS + kc]], writes=[bk[j].b], sig=(last or (j == 1 and kc == nkc - 1)))
                            if kb == nkb - 1:
                                for j in range(2):
                                    ev2(nb * 2 + j, bk[j], nb, j)
                            yield
                rmsnorm_to(TH, lambda cc: (hp.t[:, cc, :], hpB[cc]), lambda cc: (pvc("fg", cc), pvec.b), None,
                           lambda cc: (ost[:, cc, :], hidB[2 * cc]))
                for cc in range(KC):
                    hidB[2 * cc + 1].w = hidB[2 * cc].w; hidB[2 * cc + 1].r = {}
                P.dma("sp", st_sem, dr["out_fm"][:, t0:t0 + TH].rearrange("(c p) t -> p c t", p=128), ost, reads=hidB)
                yield

            cur_gen = [None]

            def filler(n=1):
                for _ in range(n):
                    if cur_gen[0] is not None:
                        try:
                            next(cur_gen[0])
                        except StopIteration:
                            cur_gen[0] = None

            def ssd_group(g, tc):
                S = sm
                sl = slice(tc * 128, (tc + 1) * 128)
                g8 = slice(g * 8, (g + 1) * 8)
                ssd_tokmajor(g, tc)
                P.op("act", lambda e: e.activation(out=stbf.t[:], in_=state.t[:, g * 512:(g + 1) * 512], func=AF.Copy),
                     reads=[stB[g]], writes=[stbf.b])
                P.op("pe", lambda e: e.matmul(out=Y1.t[:, :], lhsT=xbcT[:, IC + G + g, sl], rhs=stbf.t[:], start=True, stop=True),
                     reads=[XB[IC + G + g], stbf.b], writes=[Y1.b])
                P.op("pe", lambda e: e.matmul(out=aux0.t[:, 256:384], lhsT=xbcT[:, IC + g, sl], rhs=xbcT[:, IC + G + g, sl],
                                              start=True, stop=True), reads=[XB[IC + g], XB[IC + G + g]], writes=[aux0.b])
                P.op("dve", lambda e: e.tensor_tensor(out=cbm.t[:], in0=aux0.t[:, 256:384], in1=triu_f, op=ALU.mult),
                     reads=[aux0.b, cstf.b], writes=[cbm.b])
                filler()
                for hf in range(2):
                    h4 = slice(g * 8 + hf * 4, g * 8 + hf * 4 + 4)
                    P.op("dve", lambda e: e.tensor_tensor(out=Lb.t[:], in0=ustr_f.unsqueeze(1).to_broadcast([128, 4, 128]),
                                                          in1=S["a"].t[:, h4].unsqueeze(2).to_broadcast([128, 4, 128]), op=ALU.mult),
                         reads=[cstf.b, S["a"].b], writes=[Lb.b])
                    for hh in range(4):
                        P.op("pe", lambda e: e.matmul(out=SG.t[:, hh * 128:(hh + 1) * 128], lhsT=Lb.t[:, hh, :], rhs=triu_f,
                                                      start=True, stop=True), reads=[Lb.b, cstf.b], writes=[SG.b], sig=(hh == 3))
                    P.op("act", lambda e: e.activation(out=dec.t[:], in_=SG.t[:, :].rearrange("p (h l) -> p h l", l=128), func=AF.Exp),
                         reads=[SG.b], writes=[dec.b])
                    P.op("dve", lambda e: e.tensor_tensor(out=dec.t[:], in0=dec.t[:],
                                                          in1=S["dt"].t[:, h4].unsqueeze(2).to_broadcast([128, 4, 128]), op=ALU.mult),
                         reads=[dec.b, S["dt"].b], writes=[dec.b])
                    P.op("dve", lambda e: e.tensor_tensor(out=scT.t[:, hf * 4:hf * 4 + 4, :], in0=dec.t[:],
                                                          in1=cbm.t[:].unsqueeze(1).to_broadcast([128, 4, 128]), op=ALU.mult),
                         reads=[dec.b, cbm.b], writes=[scT.b])
                    filler()
                for hh in range(8):
                    P.op("pe", lambda e: e.matmul(out=Y0.t[:, hh * 64:(hh + 1) * 64], lhsT=scT.t[:, hh, :], rhs=xs.t[:, hh * 64:(hh + 1) * 64],
                                                  start=True, stop=True), reads=[scT.b, xs.b], writes=[Y0.b], sig=(hh == 7))
                P.op("dve", lambda e: e.tensor_tensor(out=yb.t[:].rearrange("p (h q) -> p h q", q=64),
                                                      in0=Y1.t[:, :].rearrange("p (h q) -> p h q", q=64),
                                                      in1=bc8(S["eac"].t[:, g8], 64), op=ALU.mult),
                     reads=[Y1.b, S["eac"].b], writes=[yb.b])
                P.op("dve", lambda e: e.tensor_tensor(out=yt.t[:].rearrange("p (h q) -> p h q", q=64), in0=xs.t[:, 0:512].rearrange("p (h q) -> p h q", q=64),
                                                      in1=bc8(dvec_bc[:, g8], 64), op=ALU.mult), reads=[xs.b, rowp.b], writes=[yt.b])
                P.op("dve", lambda e: e.tensor_tensor(out=yb.t[:], in0=yb.t[:], in1=yt.t[:], op=ALU.add), reads=[yb.b, yt.b], writes=[yb.b])
                filler()
                P.op("dve", lambda e: e.tensor_tensor(out=yb.t[:], in0=yb.t[:], in1=Y0.t[:, :], op=ALU.add),
                     reads=[yb.b, Y0.b], writes=[yb.b])
                P.op("dve", lambda e: e.tensor_tensor(out=yb.t[:], in0=yb.t[:], in1=zs[:, tc, g * 512:(g + 1) * 512], op=ALU.mult),
                     reads=[yb.b, ZB[tc]], writes=[yb.b])
                P.op("dve", lambda e: e.tensor_tensor(out=yt.t[:], in0=yb.t[:], in1=yb.t[:], op=ALU.mult), reads=[yb.b], writes=[yt.b])
                P.op("dve", lambda e: e.tensor_reduce(out=ssq.t[:, 0:1], in_=yt.t[:], axis=AX.X, op=ALU.add), reads=[yt.b], writes=[ssq.b])
                P.op("act", lambda e: e.activation(out=ssq.t[:, 1:2], in_=ssq.t[:, 0:1], func=AF.Sqrt, scale=1.0 / 512, bias=epsT),
                     reads=[ssq.b, misc.b], writes=[ssq.b])
                P.op("dve", lambda e: e.reciprocal(out=ssq.t[:, 1:2], in_=ssq.t[:, 1:2]), reads=[ssq.b], writes=[ssq.b])
                P.op("act", lambda e: e.activation(out=yb.t[:], in_=yb.t[:], func=AF.Copy, scale=ssq.t[:, 1:2]),
                     reads=[yb.b, ssq.b], writes=[yb.b])
                ssd_state_update(g)
                for j in range(4):
                    P.op("pe", lambda e: e.transpose(out=Y0.t[:, j * 128:(j + 1) * 128], in_=yb.t[:, j * 128:(j + 1) * 128], identity=ident_f),
                         reads=[yb.b, cstf.b], writes=[Y0.b], sig=(j == 3))
                for j in range(4):
                    P.op("act", lambda e: e.activation(out=ynT.t[:, g * 4 + j, sl], in_=Y0.t[:, j * 128:(j + 1) * 128], func=AF.Copy,
                                                       scale=pvc("ng", g * 4 + j)), reads=[Y0.b, pvec.b], writes=[YB[g * 4 + j]])
                filler()

            NTILE = T // TH
            for i in range(NTILE):
                par = i % 2
                h = h2[par]; hB = hB2[par]
                t0 = i * TH
                P.dma("sp", ld2_sem[par], h.t[:], dr["h1"][:, t0:t0 + TH].rearrange("(c p) t -> p c t", p=128), writes=hB)
                rmsnorm_to(TH, h_src(TH), lambda cc: (der.t[:, cc:cc + 1], der.b), lambda cc: (mv.t[:, cc:cc + 1], mv.b), u_dst(TH))
                mamba_inproj_xbc(TH, 0, XC)
                for nb in range(c.DI // 256):
                    wb = wload("w_z", 0, KC, nb * 256, 256)
                    for tc in range(NCH):
                        bk = banks[tc % 2]
                        for kc in range(KC):
                            P.op("pe", lambda e: e.matmul(out=bk.t[:, 0:256], lhsT=u.t[:, kc, tc * 128:(tc + 1) * 128],
                                                          rhs=wb.t[:, kc, :], start=(kc == 0), stop=(kc == KC - 1)),
                                 reads=[uB[kc], wb.b], writes=[bk.b], sig=(kc == KC - 1))
                        P.op("act", lambda e: e.activation(out=zs[:, tc, nb * 256:(nb + 1) * 256], in_=bk.t[:, 0:256], func=AF.Silu),
                             reads=[bk.b], writes=[ZB[tc]])
                for tc in range(NCH):
                    ssd_dt(0, tc)
                    for g in range(G):
                        ssd_group(g, tc)
                hcur = h; hcurB = hB

                def evo(dc, bank, nb, j):
                    P.op("dve", lambda e: e.scalar_tensor_tensor(out=hcur.t[:, dc, :], in0=bank.t[:, :TH],
                                                                 scalar=mv.t[:, 2 * KC + dc:2 * KC + dc + 1], in1=hcur.t[:, dc, :],
                                                                 op0=ALU.mult, op1=ALU.add), reads=[bank.b, mv.b, hcurB[dc]], writes=[hcurB[dc]])
                proj_A("w_out", IC, D, lambda kc: (ynT.t[:, kc, :], YB[kc]), TH, evo)
                filler(10 ** 6)
                cur_gen[0] = mlp_gen(h, hB, t0)
            filler(10 ** 6)
            P.finish("sp", hidB)
            P.eng["sp"].wait_ge(P.sems[st_sem], P.cnt[st_sem])
    nc._prog_stats = (P.nins, P.nwait)
    return nc


def _fm(v, n=None):
    v = np.asarray(v, np.float32)
    return np.ascontiguousarray(v.reshape(-1, 128).T)


def _consts():
    i = np.arange(128)
    ident = np.eye(128, dtype=np.float32)
    ones = np.ones((128, 128), np.float32)
    triu = (i[:, None] <= i[None, :]).astype(np.float32)
    ustr = (i[:, None] > i[None, :]).astype(np.float32)
    return np.ascontiguousarray(np.concatenate([ident, ones, triu, ustr], axis=1))


def host_prep(cfg, inp, n_cores):
    c = cfg
    D, KC, XC, IC, G, NH, T = c.D, c.KC, c.XC, c.IC, c.G, c.NH, c.T
    f = lambda a: np.asarray(a, np.float32)
    pv = np.zeros((128, c.NV), np.float32)

    def put(nm, arr):
        o, n = c.pv[nm]
        assert arr.shape == (128, n), (nm, arr.shape, n)
        pv[:, o:o + n] = arr
    put("nmix0", _fm(inp["norm_mix_g"][0])); put("nmlp0", _fm(inp["norm_mlp_g"][0]))
    b1 = f(inp["cf_b_pw1"][0]); put("bpa", _fm(b1[:D])); put("bpg", _fm(b1[D:]))
    put("bdw", _fm(inp["cf_b_dw"][0])); put("lng", _fm(inp["cf_ln_g"][0])); put("lnb", _fm(inp["cf_ln_b"][0]))
    put("bpw2", _fm(inp["cf_b_pw2"][0]))
    wdw = f(inp["cf_w_dw"][0])
    put("wdw", np.ascontiguousarray(wdw.T.reshape(KC, 128, 31).transpose(1, 0, 2).reshape(128, KC * 31)))
    put("bmod0", _fm(inp["b_mod"][0])); put("bmod1", _fm(inp["b_mod"][1]))
    put("nmix1", _fm(inp["norm_mix_g"][1])); put("nmlp1", _fm(inp["norm_mlp_g"][1]))
    cw = f(inp["mb_conv_w"][0])
    put("cw", np.ascontiguousarray(cw.T.reshape(XC, 128, 4).transpose(1, 0, 2).reshape(128, XC * 4)))
    put("cb", _fm(inp["mb_conv_b"][0])); put("ng", _fm(inp["mb_norm_g"][0])); put("fg", _fm(inp["final_norm_g"]))
    rowp = np.concatenate([f(inp["mb_dt_bias"][0]), f(inp["mb_a_log"][0]), f(inp["mb_d"][0])])[None, :]
    rowp = np.ascontiguousarray(rowp)
    w1 = f(inp["cf_w_pw1"][0])
    nb = 2 * D // 512
    pw1p = np.ascontiguousarray(np.concatenate(
        [np.concatenate([w1[:, 256 * i:256 * i + 256], w1[:, D + 256 * i:D + 256 * i + 256]], axis=1) for i in range(nb)], axis=1))
    w_in = f(inp["mb_w_in"][0])
    DI = c.DI
    w_z = np.ascontiguousarray(w_in[:, :DI]); w_xbc = np.ascontiguousarray(w_in[:, DI:DI + XC * 128])
    w_dt = np.ascontiguousarray(w_in[:, DI + XC * 128:])
    shared = dict(pvec=pv, rowp=rowp, cstf=_consts())
    wAB = dict(w_mod0=f(inp["w_mod"][0]), w_mod1=f(inp["w_mod"][1]), pw1p=pw1p, pw2=f(inp["cf_w_pw2"][0]),
               w1_0=f(inp["mlp_w1"][0]), w2_0=f(inp["mlp_w2"][0]), w_xbc=w_xbc, w_dt=w_dt)
    wC = dict(w_mod1=f(inp["w_mod"][1]), w_xbc=w_xbc, w_z=w_z, w_dt=w_dt, w_out=f(inp["mb_w_out"][0]),
              w1_1=f(inp["mlp_w1"][1]), w2_1=f(inp["mlp_w2"][1]))
    x = f(inp["x"]); B, S, _ = x.shape
    per_seq = S // T
    assert per_seq == c.NQ and B * per_seq == n_cores
    cores = []
    for k in range(n_cores):
        b, r = divmod(k, per_seq)
        xs_ = x[b, r * T:(r + 1) * T, :]
        halo = np.zeros((c.PRE, D), np.float32)
        if r > 0:
            halo[:] = x[b, r * T - c.PRE:r * T, :]
        cores.append(dict(b=b, r=r, x_fm=np.ascontiguousarray(xs_.T), x_halo=np.ascontiguousarray(halo.T),
                          cvec=_fm(inp["c"][b]), hmask=np.full((128, 1), 1.0 if r > 0 else 0.0, np.float32)))
    return shared, wAB, wC, cores


_NC_CACHE = {}


def _get_nc(cfg, phase):
    key = (cfg.D, cfg.G, cfg.T, cfg.NQ, phase)
    if key not in _NC_CACHE:
        _NC_CACHE[key] = build(cfg, phase)
    return _NC_CACHE[key]


def run_module(cfg, inp, n_cores):
    c = cfg
    shared, wAB, wC, cores = host_prep(cfg, inp, n_cores)
    ncA = build(cfg, "AB")
    mapsA = []
    for k in range(n_cores):
        m = dict(shared); m.update(wAB)
        m.update(x_fm=cores[k]["x_fm"], x_halo=cores[k]["x_halo"], cvec=cores[k]["cvec"], hmask=cores[k]["hmask"])
        mapsA.append(m)
    resA = run_bass_kernel_spmd(ncA, mapsA, core_ids=list(range(n_cores))).results
    ncC = build(cfg, "C")
    NQ = c.NQ
    mapsC = []
    for k in range(n_cores):
        b, r = cores[k]["b"], cores[k]["r"]
        grp = [b * NQ + j for j in range(NQ)]
        s_all = np.ascontiguousarray(np.stack([resA[j]["s_loc"] for j in grp], axis=0))
        atot_all = np.ascontiguousarray(np.concatenate([resA[j]["atot"] for j in grp], axis=1))
        inc = np.zeros((NQ, NQ), np.float32); valid = np.zeros((NQ,), np.float32)
        for j in range(NQ):
            valid[j] = 1.0 if j < r else 0.0
            for m_ in range(NQ):
                inc[j, m_] = 1.0 if (j < m_ < r) else 0.0
        m = dict(shared); m.update(wC)
        m.update(h1=resA[k]["h1"], rawhalo=resA[k]["rawhalo"], s_all=s_all, atot_all=atot_all,
                 inc=np.ascontiguousarray(np.broadcast_to(inc.reshape(1, -1), (128, NQ * NQ))),
                 valid=np.ascontiguousarray(np.broadcast_to(valid.reshape(1, -1), (128, NQ))),
                 cvec=cores[k]["cvec"], hmask=cores[k]["hmask"])
        mapsC.append(m)
    resC = run_bass_kernel_spmd(ncC, mapsC, core_ids=list(range(n_cores))).results
    B = n_cores // NQ
    out = np.empty((B, NQ * c.T, c.D), np.float32)
    for k in range(n_cores):
        b, r = cores[k]["b"], cores[k]["r"]
        out[b, r * c.T:(r + 1) * c.T, :] = resC[k]["out_fm"].T
    return out, resA, resC


def kernel(**inputs):
    cfg = Cfg()
    out, _, _ = run_module(cfg, inputs, 8)
    return out
```
